# Optimizing a Trainium2 kernel written in Bass

```python
import jax, jax.numpy as jnp
from jax import lax
import numpy as np


D_MODEL = 2048
BATCH = 8
SEQ = 2048
DEPTH = 2

CHUNK = 64
N_MIXERS = 2
GLA_HEADS = 4
GLA_DK = D_MODEL // 2 // GLA_HEADS
GLA_DV = D_MODEL // GLA_HEADS
GLA_GATE_RANK = 16
GLA_TAU = 16.0
GLA_QK = GLA_HEADS * GLA_DK
GLA_IN_COLS = 2 * GLA_QK + 2 * D_MODEL + GLA_GATE_RANK
ATT_HEADS = 16
ATT_HD = D_MODEL // ATT_HEADS
LEFT_CHUNKS = 8
BAND_CHUNKS = LEFT_CHUNKS + 1
BAND = BAND_CHUNKS * CHUNK
REL_CLIP = 256
REL_SIZE = REL_CLIP + CHUNK
D_FF = 4 * D_MODEL
EPS = 1e-6
N_GLA = (DEPTH + 1) // 2
N_ATT = DEPTH // 2

kernel_name = 'hybrid_gla_chunk_relattn_encoder'


def rmsnorm(x, g):
    xf = x.astype(jnp.float32)
    y = xf * lax.rsqrt(jnp.mean(xf * xf, axis=-1, keepdims=True) + EPS)
    return (y * g.astype(jnp.float32)).astype(x.dtype)


def gla_mixer(h, w_in, w_gate_up, b_gate, g_out, w_out):
    B, S, _ = h.shape
    N = S // CHUNK
    f32 = jnp.float32
    proj = h @ w_in
    q, k, v, r, z = jnp.split(proj, [GLA_QK, 2 * GLA_QK, 2 * GLA_QK + D_MODEL, 2 * GLA_QK + 2 * D_MODEL], axis=-1)
    logit = (z @ w_gate_up + b_gate).astype(f32)
    log_a = jax.nn.log_sigmoid(logit) / GLA_TAU

    def heads(t, d):
        return t.astype(f32).reshape(B, N, CHUNK, GLA_HEADS, d).transpose(0, 1, 3, 2, 4)

    qh = heads(q, GLA_DK) * (GLA_DK ** -0.5)
    kh = heads(k, GLA_DK)
    vh = heads(v, GLA_DV)
    b = jnp.cumsum(heads(log_a, GLA_DK), axis=3)
    b_last = b[:, :, :, -1:, :]
    q_dec = qh * jnp.exp(b)
    k_inv = kh * jnp.exp(-b)
    k_end = kh * jnp.exp(b_last - b)
    chunk_decay = jnp.exp(b_last[:, :, :, 0, :])

    causal = jnp.tril(jnp.ones((CHUNK, CHUNK), dtype=bool))
    A = jnp.where(causal, jnp.einsum('bnhcd,bnhsd->bnhcs', q_dec, k_inv), 0.0)
    o_intra = jnp.einsum('bnhcs,bnhsv->bnhcv', A, vh)

    def step(state, inp):
        q_n, k_n, v_n, dec_n = inp
        o_n = jnp.einsum('bhcd,bhdv->bhcv', q_n, state)
        state = dec_n[..., None] * state + jnp.einsum('bhcd,bhcv->bhdv', k_n, v_n)
        return state, o_n

    s0 = jnp.zeros((B, GLA_HEADS, GLA_DK, GLA_DV), f32)
    mv = lambda t: jnp.moveaxis(t, 1, 0)
    _, o_inter = lax.scan(step, s0, (mv(q_dec), mv(k_end), mv(vh), mv(chunk_decay)))
    o = o_intra + jnp.moveaxis(o_inter, 0, 1)
    o = o.transpose(0, 1, 3, 2, 4).reshape(B, S, GLA_HEADS, GLA_DV)
    o = o * lax.rsqrt(jnp.mean(o * o, axis=-1, keepdims=True) + EPS)
    o = o.reshape(B, S, D_MODEL) * g_out.astype(f32)
    o = o * jax.nn.silu(r.astype(f32))
    return (o.astype(h.dtype) @ w_out).astype(h.dtype)


def chunk_relattn_mixer(h, w_in, g_q, g_k, rel_bias, w_out):
    B, S, _ = h.shape
    N = S // CHUNK
    f32 = jnp.float32
    q, k, v = jnp.split(h @ w_in, 3, axis=-1)
    q = rmsnorm(q.reshape(B, S, ATT_HEADS, ATT_HD), g_q) * (ATT_HD ** -0.5)
    k = rmsnorm(k.reshape(B, S, ATT_HEADS, ATT_HD), g_k)
    v = v.reshape(B, S, ATT_HEADS, ATT_HD)
    pad = LEFT_CHUNKS * CHUNK
    kp = jnp.pad(k, ((0, 0), (pad, 0), (0, 0), (0, 0)))
    vp = jnp.pad(v, ((0, 0), (pad, 0), (0, 0), (0, 0)))
    c_idx = jnp.arange(CHUNK)[:, None]
    m_idx = jnp.arange(BAND)[None, :]
    dist = pad + c_idx - m_idx
    rel_idx = jnp.clip(dist, -(CHUNK - 1), REL_CLIP) + (CHUNK - 1)
    bias = rel_bias[:, rel_idx].astype(f32)
    key_pos = jnp.arange(BAND)
    qc = jnp.moveaxis(q.reshape(B, N, CHUNK, ATT_HEADS, ATT_HD), 1, 0)

    def one_chunk(args):
        q_n, n = args
        k_n = lax.dynamic_slice_in_dim(kp, n * CHUNK, BAND, axis=1)
        v_n = lax.dynamic_slice_in_dim(vp, n * CHUNK, BAND, axis=1)
        s = jnp.einsum('bchd,bmhd->bhcm', q_n, k_n).astype(f32) + bias
        valid = key_pos >= (LEFT_CHUNKS - n) * CHUNK
        s = jnp.where(valid, s, -jnp.inf)
        p = jax.nn.softmax(s, axis=-1)
        return jnp.einsum('bhcm,bmhd->bchd', p.astype(v_n.dtype), v_n)

    out = lax.map(one_chunk, (qc, jnp.arange(N)))
    out = jnp.moveaxis(out, 0, 1).reshape(B, S, D_MODEL)
    return (out @ w_out).astype(h.dtype)


def sqrelu_mlp(h, w_up, w_down):
    return jnp.square(jax.nn.relu(h @ w_up)) @ w_down


def setup_inputs(seed: int = 0) -> dict:
    key = jax.random.key(seed)
    ks = jax.random.split(key, 16)
    f32 = jnp.float32

    def nrm(k, shape, scale):
        return jax.random.normal(k, shape, f32) * scale

    return {
        'x': nrm(ks[0], (BATCH, SEQ, D_MODEL), 1.0),
        'norm_mix_g': 1.0 + nrm(ks[1], (DEPTH, D_MODEL), 0.02),
        'norm_mlp_g': 1.0 + nrm(ks[2], (DEPTH, D_MODEL), 0.02),
        'gla_w_in': nrm(ks[3], (N_GLA, D_MODEL, GLA_IN_COLS), D_MODEL ** -0.5),
        'gla_w_gate_up': nrm(ks[4], (N_GLA, GLA_GATE_RANK, GLA_QK), GLA_GATE_RANK ** -0.5),
        'gla_b_gate': nrm(ks[5], (N_GLA, GLA_QK), 0.1),
        'gla_g_out': 1.0 + nrm(ks[6], (N_GLA, D_MODEL), 0.02),
        'gla_w_out': nrm(ks[7], (N_GLA, D_MODEL, D_MODEL), D_MODEL ** -0.5),
        'att_w_in': nrm(ks[8], (N_ATT, D_MODEL, 3 * D_MODEL), D_MODEL ** -0.5),
        'att_g_q': 1.0 + nrm(ks[9], (N_ATT, ATT_HD), 0.02),
        'att_g_k': 1.0 + nrm(ks[10], (N_ATT, ATT_HD), 0.02),
        'att_rel_bias': nrm(ks[11], (N_ATT, ATT_HEADS, REL_SIZE), 0.2),
        'att_w_out': nrm(ks[12], (N_ATT, D_MODEL, D_MODEL), D_MODEL ** -0.5),
        'mlp_w_up': nrm(ks[13], (DEPTH, D_MODEL, D_FF), D_MODEL ** -0.5),
        'mlp_w_down': nrm(ks[14], (DEPTH, D_FF, D_MODEL), D_FF ** -0.5),
    }


def reference(x, norm_mix_g, norm_mlp_g, gla_w_in, gla_w_gate_up, gla_b_gate, gla_g_out, gla_w_out,
              att_w_in, att_g_q, att_g_k, att_rel_bias, att_w_out, mlp_w_up, mlp_w_down):
    for i in range(DEPTH):
        h = rmsnorm(x, norm_mix_g[i])
        j = i // N_MIXERS
        if i % N_MIXERS == 0:
            mix = gla_mixer(h, gla_w_in[j], gla_w_gate_up[j], gla_b_gate[j], gla_g_out[j], gla_w_out[j])
        else:
            mix = chunk_relattn_mixer(h, att_w_in[j], att_g_q[j], att_g_k[j], att_rel_bias[j], att_w_out[j])
        x = x + mix.astype(x.dtype)
        x = x + sqrelu_mlp(rmsnorm(x, norm_mlp_g[i]), mlp_w_up[i], mlp_w_down[i]).astype(x.dtype)
    return x
```

```python
import contextlib
import numpy as np
import concourse.bass as bass
import concourse.mybir as mybir
from concourse.bass_utils import run_bass_kernel_spmd

F32 = mybir.dt.float32
BF16 = mybir.dt.bfloat16
AF = mybir.ActivationFunctionType
ALU = mybir.AluOpType

S = 2048
D = 2048
NCH = 16
TB = 1024
NTB = S // TB
NTH = TB // 512
DFF = 8192
EPS = 1e-6
ROLL = 30000
NSLOT = 8
NWSLOT = 4
GLA_COLS = 6160
NEG = -30000.0
LN_QS = float(np.log(256.0 ** -0.5))


class Op:
    __slots__ = ("eng", "fn", "waits", "dwaits", "isdma", "sig", "sigcount", "slot", "slotval", "idx", "vc", "vd", "k")


class Sched:
    ENGS = ("pe", "act", "dve", "pool", "sp")

    def __init__(self):
        self.ops = {e: [] for e in self.ENGS}
        self.lastw = {}
        self.readers = {}
        self.known = {e: {} for e in self.ENGS}
        self.knownd = {e: {} for e in self.ENGS}
        self.ndma = {"sp": 0, "pool": 0}
        self.lastc = {}
        self.lastd = {}

    def _finish(self, op, deps):
        eng = op.eng
        kn = self.known[eng]
        kd = self.knownd[eng]
        waits = {}
        dwaits = {}
        for d in deps:
            if d is op:
                continue
            if d.isdma:
                key = (d.eng, d.slot)
                if kd.get(key, 0) < d.slotval:
                    kd[key] = d.slotval
                    dwaits[key] = max(dwaits.get(key, 0), d.slotval)
                    self._merge(kn, kd, d)
            else:
                if d.eng == "pe" and eng == "pe":
                    continue
                if kn.get(d.eng, -1) < d.idx:
                    kn[d.eng] = d.idx
                    old = waits.get(d.eng)
                    if old is None or old.idx < d.idx:
                        waits[d.eng] = d
                    self._merge(kn, kd, d)
        op.waits = waits
        op.dwaits = dwaits
        op.vc = dict(kn)
        op.vd = dict(kd)
        op.sig = False
        op.sigcount = 0
        op.idx = len(self.ops[eng])
        self.ops[eng].append(op)

    @staticmethod
    def _merge(kn, kd, d):
        for f, i in d.vc.items():
            if kn.get(f, -1) < i:
                kn[f] = i
        for f, v in d.vd.items():
            if kd.get(f, 0) < v:
                kd[f] = v

    def add(self, eng, fn, reads=(), writes=(), dma=False):
        op = Op()
        op.eng = eng
        op.fn = fn
        op.isdma = dma
        op.slot = 0
        op.slotval = 0
        op.k = 0
        if dma:
            k = self.ndma[eng]
            self.ndma[eng] = k + 1
            op.k = k
            op.slot = k % NSLOT
            op.slotval = 16 * (k // NSLOT + 1)
        deps = []
        for r in reads:
            w = self.lastw.get(r)
            if w is not None:
                deps.append(w)
        for x in writes:
            w = self.lastw.get(x)
            if w is not None:
                deps.append(w)
            deps.extend(self.readers.get(x, ()))
        self._finish(op, deps)
        for r in reads:
            self.readers.setdefault(r, []).append(op)
        for x in writes:
            self.lastw[x] = op
            self.readers[x] = []
        if dma:
            self.lastd[(eng, op.slot)] = op
        else:
            self.lastc[eng] = op
        return op

    def barrier(self):
        deps = list(self.lastc.values()) + list(self.lastd.values())
        for e in self.ENGS:
            op = Op()
            op.eng = e
            op.fn = None
            op.isdma = False
            op.slot = 0
            op.slotval = 0
            op.k = 0
            self._finish(op, deps)
        self.lastw.clear()
        self.readers.clear()

    def emit(self, nc, stack):
        for e in self.ENGS:
            for op in self.ops[e]:
                for d in op.waits.values():
                    d.sig = True
        nsig = {}
        for e in self.ENGS:
            c = 0
            for op in self.ops[e]:
                if op.sig:
                    c += 1
                    op.sigcount = c
            nsig[e] = c
        engsem = {}
        for e in self.ENGS:
            n = max(1, (nsig[e] + ROLL - 1) // ROLL)
            engsem[e] = [stack.enter_context(nc.semaphore("s_%s_%d" % (e, i))) for i in range(n)]
        dmasem = {q: [stack.enter_context(nc.semaphore("d_%s_%d" % (q, i))) for i in range(NSLOT)] for q in ("sp", "pool")}
        block = stack.enter_context(nc.Block())
        ndma = self.ndma

        def run(name, eng):
            for op in self.ops[name]:
                for f, d in op.waits.items():
                    k, v = divmod(d.sigcount - 1, ROLL)
                    eng.wait_ge(engsem[f][k], v + 1)
                for (q, slot), val in op.dwaits.items():
                    eng.wait_ge(dmasem[q][slot], val)
                if op.fn is None:
                    continue
                if op.isdma:
                    if op.k >= NSLOT:
                        eng.wait_ge(dmasem[name][op.slot], op.slotval - 16)
                    ins = op.fn(eng)
                    ins.then_inc(dmasem[name][op.slot], 16)
                else:
                    ins = op.fn(eng)
                    if op.sig:
                        ins.then_inc(engsem[name][(op.sigcount - 1) // ROLL], 1)
            if name == "sp":
                for q in ("sp", "pool"):
                    for slot in range(NSLOT):
                        n = (ndma[q] - slot + NSLOT - 1) // NSLOT
                        if n > 0:
                            eng.wait_ge(dmasem[q][slot], 16 * n)

        @block.tensor
        def _(e):
            run("pe", e)

        @block.scalar
        def _(e):
            run("act", e)

        @block.vector
        def _(e):
            run("dve", e)

        @block.gpsimd
        def _(e):
            run("pool", e)

        @block.sync
        def _(e):
            run("sp", e)


def build(debug=False, stop_after=99, only=None, opts=()):
    nc = bass.Bass("TRN2", target_bir_lowering=False)
    sc = Sched()
    stack = contextlib.ExitStack()
    uctr = [0]

    def sbt(name, shape, dt):
        uctr[0] += 1
        return nc.sbuf_tensor("%s_u%d" % (name, uctr[0]), shape, dt)

    def din(name, shape, dt=F32):
        if only is not None and int(np.prod(shape)) > (1 << 20):
            shape = [1] * (len(shape) - 1) + [16]
        return nc.dram_tensor(name, list(shape), dt, kind="ExternalInput")

    skind = "ExternalOutput" if debug else "Internal"

    def dscr(name, shape, dt=F32):
        kind = skind
        if only == "gla" and name in ("QT", "KT", "RT", "ZT", "VTOK"):
            kind = "ExternalInput"
        if only == "att" and name in ("QN", "KN", "VTOK"):
            kind = "ExternalInput"
        return nc.dram_tensor(name, list(shape), dt, kind=kind)

    x_in = din("x", [S, D]).ap()
    y_out = nc.dram_tensor("y", [S, D], F32, kind="ExternalOutput").ap()
    gla_w_in = din("gla_w_in", [D, GLA_COLS]).ap()
    gla_w_gate = din("gla_w_gate_up", [16, 1024]).ap()
    gla_b_gate = din("gla_b_gate", [1, 1024]).ap()
    gla_w_out = din("gla_w_out", [D, D]).ap()
    att_w_in = din("att_w_in", [D, 3 * D]).ap()
    att_w_out = din("att_w_out", [D, D]).ap()
    mlp_w_up = din("mlp_w_up", [2, D, DFF]).ap()
    mlp_w_down = din("mlp_w_down", [2, DFF, D]).ap()
    extr_t = din("att_ext", [16, 768])
    gcols_d = din("gcols", [128, 82]).ap()
    ident_d = din("c_ident", [128, 128]).ap()
    anti_d = din("c_anti", [128, 128]).ap()
    ltri_d = din("c_ltri", [128, 128]).ap()
    cmaskA_d = din("c_maskA", [128, 512]).ap()
    cmaskB_d = din("c_maskB", [128, 640]).ap()
    onesrow_d = din("c_onesrow", [1, S]).ap()

    XT = dscr("XT", [NCH, 128, S]).ap()
    QT = dscr("QT", [16, 128, 8, 128], BF16).ap()
    KT = dscr("KT", [16, 128, 8, 128], BF16).ap()
    RT = dscr("RT", [16, 128, NCH, 128], BF16).ap()
    ZT = dscr("ZT", [16, S]).ap()
    VTOK = dscr("VTOK", [S, D], BF16).ap()
    OT = dscr("OT", [NCH, 128, S], BF16).ap()
    QN = dscr("QN", [NCH, 128, S], BF16).ap()
    KN = dscr("KN", [NCH, 128, S], BF16).ap()

    def sb(name, shape, dt=F32):
        return stack.enter_context(sbt("sb_" + name, list(shape), dt))

    banks = [stack.enter_context(nc.psum_tensor("ps%d" % i, [128, 512], F32)) for i in range(8)]
    bank_ctr = [0]

    def nb():
        b = bank_ctr[0] % 8
        bank_ctr[0] += 1
        return b

    wslots = [sb("w%d" % i, [128, 16, 256], BF16) for i in range(NWSLOT)]
    gcols = sb("gcols", [128, 82])
    ident = sb("ident", [128, 128])
    ident_bf = sb("ident_bf", [128, 128], BF16)
    anti_bf = sb("anti_bf", [128, 128], BF16)
    ones_d = sb("ones_d", [128, 128], BF16)
    ones_512 = sb("ones_512", [128, 128], BF16)
    ones_128 = sb("ones_128", [128, 128], BF16)
    ones_1 = sb("ones_1", [128, 128], BF16)
    tmpc = sb("tmpc", [128, 128])
    gq_s = sb("gq_s", [128, 1])
    epsc = sb("epsc", [128, 1])

    wlist = []
    wstate = {"issued": 0, "used": 0}

    def wreg(key, ap, ncols=256):
        wlist.append((key, ap, ncols))

    def w_issue_upto(n):
        while wstate["issued"] < min(n, len(wlist)):
            i = wstate["issued"]
            key, ap, ncols = wlist[i]
            slot = i % NWSLOT
            dst = wslots[slot][:, :, 0:ncols]
            src = ap.rearrange("(k p) m -> p k m", p=128)
            sc.add("pool", lambda e, dst=dst, src=src: e.dma_start(out=dst, in_=src),
                   writes=[("w", slot)], dma=True)
            wstate["issued"] += 1

    def wnext(key):
        i = wstate["used"]
        assert wlist[i][0] == key, (wlist[i][0], key)
        w_issue_upto(i + NWSLOT)
        wstate["used"] += 1
        return i % NWSLOT

    for l in range(2 if only is None else 0):
        w_in = gla_w_in if l == 0 else att_w_in
        npc = 24
        for tb in range(NTB):
            for pc in range(npc):
                wreg(("in", l, tb, pc), w_in[:, pc * 256:(pc + 1) * 256])
            if l == 0:
                wreg(("in", l, tb, 24), w_in[:, 6144:6160], 16)
        w_o = gla_w_out if l == 0 else att_w_out
        for tb in range(NTB):
            for pc in range(8):
                wreg(("out", l, tb, pc), w_o[:, pc * 256:(pc + 1) * 256])
            for q in range(4):
                for pc in range(8):
                    c0 = q * 2048 + pc * 256
                    wreg(("up", l, tb, q, pc), mlp_w_up[l, :, c0:c0 + 256])
                for pc in range(8):
                    wreg(("dn", l, tb, q, pc), mlp_w_down[l, q * 2048:(q + 1) * 2048, pc * 256:(pc + 1) * 256])

    def mm(out, pairs, reads, bank, extra_writes=()):
        def fn(pe, out=out, pairs=pairs):
            n = len(pairs)
            ins = None
            for i, (l, r) in enumerate(pairs):
                ins = pe.matmul(out, lhsT=l, rhs=r, start=(i == 0), stop=(i == n - 1))
            return ins
        return sc.add("pe", fn, reads=reads, writes=[("bank", bank)] + list(extra_writes))

    def dma(q, out, in_, reads=(), writes=()):
        return sc.add(q, lambda e, out=out, in_=in_: e.dma_start(out=out, in_=in_), reads=reads, writes=writes, dma=True)

    def act(out, in_, func, reads, writes, bias=None, scale=None):
        def fn(e, out=out, in_=in_, func=func, bias=bias, scale=scale):
            kw = {}
            if bias is not None:
                kw["bias"] = bias
            if scale is not None:
                kw["scale"] = scale
            return e.activation(out=out, in_=in_, func=func, **kw)
        return sc.add("act", fn, reads=reads, writes=writes)

    def dve(fn, reads, writes):
        return sc.add("dve", fn, reads=reads, writes=writes)

    dma("sp", gcols[:], gcols_d, writes=["gcols"])
    dma("sp", ident[:], ident_d, writes=["ident"])
    dma("sp", tmpc[:], anti_d, writes=["tmpc"])
    dve(lambda e: e.tensor_copy(out=ident_bf[:], in_=ident[:]), ["ident"], ["ident_bf"])
    dve(lambda e: e.tensor_copy(out=anti_bf[:], in_=tmpc[:]), ["tmpc"], ["anti_bf"])
    dve(lambda e: e.memset(ones_d[:], 1.0 / D), [], ["ones_d"])
    dve(lambda e: e.memset(ones_512[:], 1.0 / 512), [], ["ones_512"])
    dve(lambda e: e.memset(ones_128[:], 1.0 / 128), [], ["ones_128"])
    dve(lambda e: e.memset(ones_1[:], 1.0), [], ["ones_1"])
    dve(lambda e: e.memset(epsc[:], EPS), [], ["epsc"])
    dve(lambda e: e.tensor_scalar(out=gq_s[:], in0=gcols[:, 80:81], scalar1=128.0 ** -0.5, scalar2=None, op0=ALU.mult),
        ["gcols"], ["gq_s"])
    sc.barrier()

    GC_MIX = [0, 32]
    GC_MLP = [16, 48]
    GC_GOUT = 64
    GC_GK = 81

    def phase_A():
        with contextlib.ExitStack() as st:
            xt8 = [st.enter_context(sbt("xa%d" % i, [128, D], F32)) for i in range(8)]
            stg = [st.enter_context(sbt("xs%d" % i, [128, 8, 512], F32)) for i in range(2)]
            for g in range(S // 512):
                xt = xt8[(g % 2) * 4:(g % 2) * 4 + 4]
                for i in range(4):
                    r0 = g * 512 + i * 128
                    dma("sp", xt[i][:], x_in[r0:r0 + 128, :], writes=[("xa", g % 2, i)])
                for hh in range(2):
                    sg = stg[hh]
                    for c8 in range(8):
                        c = hh * 8 + c8
                        b = nb()

                        def fn(pe, b=b, c=c, xt=xt):
                            ins = None
                            for i in range(4):
                                ins = pe.transpose(out=banks[b][:, i * 128:(i + 1) * 128],
                                                   in_=xt[i][:, c * 128:(c + 1) * 128], identity=ident[:])
                            return ins
                        sc.add("pe", fn, reads=[("xa", g % 2, i) for i in range(4)], writes=[("bank", b)])
                        if c % 2 == 0:
                            act(sg[:, c8, :], banks[b][:], AF.Copy, [("bank", b)], [("xs", hh, c8)])
                        else:
                            dve(lambda e, b=b, sg=sg, c8=c8: e.tensor_copy(out=sg[:, c8, :], in_=banks[b][:]),
                                [("bank", b)], [("xs", hh, c8)])
                    dma("pool", XT[hh * 8:(hh + 1) * 8, :, g * 512:(g + 1) * 512].rearrange("c p t -> p c t"), sg[:],
                        reads=[("xs", hh, c8) for c8 in range(8)], writes=[("XT", g, hh)])
        sc.barrier()

    def norm_fm(x_fm, hT, gbase, sq, rstd):
        for th in range(NTH):
            ts = slice(th * 512, (th + 1) * 512)
            b = nb()
            for c in range(NCH):
                s_ = c % 2
                act(sq[s_][:], x_fm[:, c, ts], AF.Square, [("x", c, th)], [("sq", s_)])
                sc.add("pe", lambda pe, b=b, s_=s_, c=c: pe.matmul(banks[b][:], lhsT=ones_d[:], rhs=sq[s_][:],
                                                                     start=(c == 0), stop=(c == NCH - 1)),
                       reads=[("sq", s_)], writes=[("bank", b)])
            act(rstd[:, ts], banks[b][:], AF.Ln, [("bank", b)], [("rstd", th)], bias=EPS)
            act(rstd[:, ts], rstd[:, ts], AF.Exp, [("rstd", th)], [("rstd", th)], scale=-0.5)
            for c in range(NCH):
                dve(lambda e, c=c, ts=ts: e.scalar_tensor_tensor(out=hT[:, c, ts], in0=x_fm[:, c, ts],
                                                                  scalar=gcols[:, gbase + c:gbase + c + 1],
                                                                  in1=rstd[:, ts], op0=ALU.mult, op1=ALU.mult),
                    [("x", c, th), ("rstd", th)], [("act", c, th)])

    def phase_P1(l):
        with contextlib.ExitStack() as st:
            x_fm = st.enter_context(sbt("p1x", [128, NCH, TB], F32))
            hT = st.enter_context(sbt("p1h", [128, NCH, TB], BF16))
            vst = st.enter_context(sbt("p1v", [128, 8, D], BF16))
            sq = [st.enter_context(sbt("p1sq%d" % i, [128, 512], BF16)) for i in range(2)]
            rstd = st.enter_context(sbt("p1rs", [128, TB], F32))
            stg = [st.enter_context(sbt("p1st%d" % i, [128, 512], F32)) for i in range(4)]
            stb = [st.enter_context(sbt("p1sb%d" % i, [128, 512], BF16)) for i in range(4)]
            rs2 = [st.enter_context(sbt("p1r2%d" % i, [128, 512], F32)) for i in range(2)]
            stc = [0]
            stp = [0]
            stb2 = [st.enter_context(sbt("p1s2%d" % i, [128, 2, 512], BF16)) for i in range(2)]
            pending = []

            def flush():
                while pending:
                    pending.pop(0)()
            def load_x(tb_):
                for cg in range(4):
                    dma("sp", x_fm[:, cg * 4:(cg + 1) * 4, :], XT[cg * 4:(cg + 1) * 4, :, tb_ * TB:(tb_ + 1) * TB].rearrange("c p t -> p c t"),
                        writes=[("x", c, th) for c in range(cg * 4, cg * 4 + 4) for th in range(NTH)])
            load_x(0)
            for tb in range(NTB):
                t0 = tb * TB
                norm_fm(x_fm, hT, GC_MIX[l], sq, rstd)
                if tb + 1 < NTB:
                    load_x(tb + 1)
                hreads = {th: [("act", k, th) for k in range(NCH)] for th in range(NTH)}
                allh = hreads[0] + hreads[1]
                npc = 25 if l == 0 else 24
                for pc in range(npc):
                    ws = wnext(("in", l, tb, pc))
                    W = wslots[ws]
                    if l == 0:
                        kind = "q" if pc < 4 else "k" if pc < 8 else "v" if pc < 16 else "r" if pc < 24 else "z"
                    else:
                        kind = "q" if pc < 8 else "k" if pc < 16 else "v"
                    if kind == "v":
                        flush()
                        g = pc - (8 if l == 0 else 16)
                        for tt in range(8):
                            b = nb()
                            mm(banks[b][:, 0:256], [(hT[:, k, tt * 128:(tt + 1) * 128], W[:, k, :]) for k in range(NCH)],
                               allh + [("w", ws)], b)
                            if tt % 2 == 0:
                                act(vst[:, tt, g * 256:(g + 1) * 256], banks[b][:, 0:256], AF.Copy, [("bank", b)], [("vst", tt, g)])
                            else:
                                dve(lambda e, b=b, tt=tt, g=g: e.tensor_copy(out=vst[:, tt, g * 256:(g + 1) * 256], in_=banks[b][:, 0:256]),
                                    [("bank", b)], [("vst", tt, g)])
                        if g == 7:
                            dma("sp", VTOK[t0:t0 + TB, :].rearrange("(t p) d -> p t d", p=128), vst[:],
                                reads=[("vst", tt, gg) for tt in range(8) for gg in range(8)], writes=[("VTOK", tb)])
                        continue
                    if kind == "z":
                        for th in range(NTH):
                            ts = slice(th * 512, (th + 1) * 512)
                            b = nb()
                            mm(banks[b][0:16, :], [(W[:, k, 0:16], hT[:, k, ts]) for k in range(NCH)], hreads[th] + [("w", ws)], b)
                            s_ = stc[0] % 4
                            stc[0] += 1
                            act(stg[s_][0:16, :], banks[b][0:16, :], AF.Copy, [("bank", b)], [("stg", s_)])
                            dma("sp", ZT[:, t0 + th * 512:t0 + (th + 1) * 512], stg[s_][0:16, :], reads=[("stg", s_)], writes=[("ZT", tb, th)])
                        continue
                    for th, cc in ([(th, cc) for th in range(NTH) for cc in range(2)] if l == 0 else [(th, cc) for cc in range(2) for th in range(NTH)]):
                        if True:
                            ts = slice(th * 512, (th + 1) * 512)
                            tsl = slice(t0 + th * 512, t0 + (th + 1) * 512)
                            b = nb()
                            mm(banks[b][:], [(W[:, k, cc * 128:(cc + 1) * 128], hT[:, k, ts]) for k in range(NCH)],
                               hreads[th] + [("w", ws)], b)
                            s_ = stc[0] % 4
                            stc[0] += 1
                            if l == 0:
                                base = {"q": 0, "k": 4, "r": 16}[kind]
                                ch = (pc - base) * 2 + cc
                                dst = {"q": QT, "k": KT, "r": RT}[kind]
                                if cc == 0:
                                    stp[0] = (stp[0] + 1) % 2
                                q_ = stp[0]
                                so = stb2[q_][:, cc, :]
                                if kind == "r":
                                    act(stg[s_][:], banks[b][:], AF.Silu, [("bank", b)], [("stg", s_)])
                                    gc = GC_GOUT + ch
                                    act(so, stg[s_][:], AF.Copy, [("stg", s_)], [("stb2", q_, cc)], scale=gcols[:, gc:gc + 1])
                                elif stc[0] % 2 == 0:
                                    act(so, banks[b][:], AF.Copy, [("bank", b)], [("stb2", q_, cc)])
                                else:
                                    dve(lambda e, b=b, so=so: e.tensor_copy(out=so, in_=banks[b][:]), [("bank", b)], [("stb2", q_, cc)])
                                if cc == 1:
                                    tile0 = (t0 + th * 512) // 128
                                    for k4 in range(4):
                                        dma("sp", dst[tile0 + k4, :, ch - 1:ch + 1, :], stb2[q_][:, :, k4 * 128:(k4 + 1) * 128],
                                            reads=[("stb2", q_, 0), ("stb2", q_, 1)], writes=[(kind, ch, tb, th, k4)])
                            else:
                                base = {"q": 0, "k": 8}[kind]
                                hd = (pc - base) * 2 + cc
                                dst = QN if kind == "q" else KN
                                gcol = gq_s[:, 0:1] if kind == "q" else gcols[:, GC_GK:GC_GK + 1]
                                q_ = stc[0] % 2
                                act(sq[q_][:], banks[b][:], AF.Square, [("bank", b)], [("sq", q_)])
                                flush()

                                def epi(b=b, q_=q_, gcol=gcol, dst=dst, hd=hd, tsl=tsl, kind=kind, th=th):
                                    b2 = nb()
                                    rs = rs2[q_]
                                    mm(banks[b2][:], [(ones_128[:], sq[q_][:])], [("sq", q_)], b2)
                                    act(rs[:], banks[b2][:], AF.Ln, [("bank", b2)], [("rs2", q_)], bias=EPS)
                                    act(rs[:], rs[:], AF.Exp, [("rs2", q_)], [("rs2", q_)], scale=-0.5)
                                    dve(lambda e: e.scalar_tensor_tensor(out=stb[q_][:], in0=banks[b][:], scalar=gcol,
                                                                         in1=rs[:], op0=ALU.mult, op1=ALU.mult),
                                        [("bank", b), ("rs2", q_)], [("stb", q_)])
                                    dma("sp", dst[hd, :, tsl], stb[q_][:], reads=[("stb", q_)], writes=[(kind, hd, tb, th)])
                                pending.append(epi)
                flush()
        sc.barrier()

    def phase_GLA():
        with contextlib.ExitStack() as st:
            def t(name, shape, dt=F32):
                return st.enter_context(sbt(name, list(shape), dt))

            def t2(name, shape, dt=F32):
                return [t("%s%d" % (name, i), shape, dt) for i in range(2)]
            oT4 = t2("g_oT", [128, NCH, 512], BF16)
            st32 = t("g_st32", [128, 8, 512])
            stbf = t("g_stbf", [128, 8, 512], BF16)
            zt = [t("g_zT%d" % i, [17, 128]) for i in range(3)]
            wg = t("g_wg", [17, 1024])
            ltri = t("g_ltri", [128, 128])
            mA = t("g_mA", [128, 512])
            sp32s = [t("g_sp%d" % i, [128, 1024]) for i in range(3)]
            ebs = [t("g_eb%d" % i, [128, 8, 128]) for i in range(3)]
            enbs = [t("g_enb%d" % i, [128, 8, 128]) for i in range(3)]
            q32 = [t("g_q%d" % i, [128, 8, 128], BF16) for i in range(3)]
            k32 = [t("g_k%d" % i, [128, 8, 128], BF16) for i in range(3)]
            vt = [t("g_v%d" % i, [128, D], BF16) for i in range(3)]
            r32 = [t("g_r%d" % i, [128, NCH, 128], BF16) for i in range(3)]
            qdecs = t2("g_qd", [128, 8, 128], BF16)
            kinvs = t2("g_ki", [128, 8, 128], BF16)
            kends = t2("g_ke", [128, 8, 128], BF16)
            kendTs = t2("g_keT", [128, 1024], BF16)
            atbfs = t2("g_at", [128, 4, 128], BF16)
            osq = t2("g_os", [128, 4, 128], BF16)
            rstds = t2("g_rs", [128, 128])
            t1s = t2("g_t1", [128, 4, 128], BF16)

            for i in range(3):
                dma("sp", zt[i][16:17, :], onesrow_d[:, 0:128], writes=[("zT1", i)])
            dma("sp", wg[0:16, :], gla_w_gate, writes=["wg"])
            dma("sp", wg[16:17, :], gla_b_gate, writes=["wg1"])
            dma("sp", ltri[:], ltri_d, writes=["ltri"])
            dma("sp", mA[:], cmaskA_d, writes=["mA"])
            NT = S // 128
            hc = [0]

            def loads(tt):
                tsl = slice(tt * 128, (tt + 1) * 128)
                lp = tt % 3
                dma("sp", zt[lp][0:16, :], ZT[:, tsl], writes=[("zT", lp)])
                dma("sp", q32[lp][:], QT[tt], writes=[("q32", lp)])
                dma("sp", k32[lp][:], KT[tt], writes=[("k32", lp)])
                dma("sp", vt[lp][:], VTOK[tsl, :], writes=[("vt", lp)])
                dma("sp", r32[lp][:], RT[tt], writes=[("r32", lp)])

            def front_stages(tt):
                p = tt % 2
                lp = tt % 3
                e3 = tt % 3
                sp32, eb, enb, qdec, kinv, kend, kendT, atbf = sp32s[e3], ebs[e3], enbs[e3], qdecs[p], kinvs[p], kends[p], kendTs[p], atbfs[p]

                def F1():
                    bl = [0, 1]
                    for hf in range(2):
                        hsl = slice(hf * 512, (hf + 1) * 512)
                        mm(banks[bl[hf]][:], [(zt[lp][0:17, :], wg[0:17, hsl])], [("zT", lp), ("zT1", lp), "wg", "wg1"], bl[hf])
                        act(sp32[:, hsl], banks[bl[hf]][:], AF.Exp, [("bank", bl[hf])], [("sp", e3, hf)], scale=-1.0)
                        act(sp32[:, hsl], sp32[:, hsl], AF.Ln, [("sp", e3, hf)], [("sp", e3, hf)], bias=1.0)

                def F2():
                    bb = [0, 1]
                    for hf in range(2):
                        def fn(pe, hf=hf, b=bb[hf]):
                            ins = None
                            for j in range(4):
                                dch = hf * 4 + j
                                ins = pe.matmul(banks[b][:, j * 128:(j + 1) * 128], lhsT=sp32[:, dch * 128:(dch + 1) * 128], rhs=ltri[:],
                                                start=True, stop=True)
                            return ins
                        sc.add("pe", fn, reads=[("sp", e3, hf), "ltri"], writes=[("bank", bb[hf])])
                        bv = banks[bb[hf]][:].rearrange("p (j t) -> p j t", j=4)
                        hs = slice(hf * 4, (hf + 1) * 4)
                        act(eb[:, hs, :], bv, AF.Exp, [("bank", bb[hf])], [("eb", e3, hf)])
                        act(enb[:, hs, :], bv, AF.Exp, [("bank", bb[hf])], [("enb", e3, hf)], scale=-1.0)

                def F3q():
                    dve(lambda e: e.scalar_tensor_tensor(out=qdec[:], in0=q32[lp][:], scalar=0.0625, in1=eb[:], op0=ALU.mult, op1=ALU.mult),
                        [("q32", lp), ("eb", e3, 0), ("eb", e3, 1)], [("qdec", p)])

                def F3k():
                    dve(lambda e: e.tensor_tensor(out=kinv[:], in0=k32[lp][:], in1=enb[:], op=ALU.mult),
                        [("k32", lp), ("enb", e3, 0), ("enb", e3, 1)], [("kinv", p)])

                def F3b():
                    for dch in range(8):
                        dve(lambda e, dch=dch: e.scalar_tensor_tensor(out=kend[:, dch, :], in0=k32[lp][:, dch, :], scalar=eb[:, dch, 127:128],
                                                                       in1=enb[:, dch, :], op0=ALU.mult, op1=ALU.mult),
                            [("k32", lp), ("eb", e3, dch // 4), ("enb", e3, dch // 4)], [("kend", p)])

                def F4():
                    bt = 2

                    def fnT(pe):
                        ins = None
                        o = banks[bt][:].bitcast(BF16)
                        for dch in range(8):
                            ins = pe.transpose(out=o[:, dch * 128:(dch + 1) * 128], in_=kend[:, dch, :], identity=ident_bf[:])
                        return ins
                    sc.add("pe", fnT, reads=[("kend", p)], writes=[("bank", bt)])
                    act(kendT[:], banks[bt][:].bitcast(BF16), AF.Copy, [("bank", bt)], [("kendT", p)])
                    ba = 2

                    def fnA(pe):
                        ins = None
                        for h in range(4):
                            for i in range(2):
                                ins = pe.matmul(banks[ba][:, h * 128:(h + 1) * 128], lhsT=kinv[:, 2 * h + i, :], rhs=qdec[:, 2 * h + i, :],
                                                start=(i == 0), stop=(i == 1))
                        return ins
                    sc.add("pe", fnA, reads=[("kinv", p), ("qdec", p)], writes=[("bank", ba)])
                    dve(lambda e: e.tensor_tensor(out=atbf[:].rearrange("p h c -> p (h c)"), in0=banks[ba][:], in1=mA[:], op=ALU.mult),
                        [("bank", ba), "mA"], [("atbf", p)])
                return dict(F1=F1, F2=F2, F3q=F3q, F3k=F3k, F3b=F3b, F4=F4)

            def back_head(tt, h):
                tsl = slice(tt * 128, (tt + 1) * 128)
                p = tt % 2
                lp = tt % 3
                e3 = tt % 3
                eb, qdec, kendT, atbf = ebs[e3], qdecs[p], kendTs[p], atbfs[p]
                og = (tt // 4) % 2
                osl = slice((tt % 4) * 128, (tt % 4 + 1) * 128)
                hp = hc[0] % 2
                hc[0] += 1
                rstd, t1 = rstds[hp], t1s[hp]
                bo = 3 + hp

                def fnO(pe):
                    ins = None
                    for j in range(4):
                        o = banks[bo][:, j * 128:(j + 1) * 128]
                        ins = pe.matmul(o, lhsT=vt[lp][:, h * 512 + j * 128:h * 512 + (j + 1) * 128], rhs=atbf[:, h, :],
                                        start=True, stop=(tt == 0))
                        if tt > 0:
                            for i in range(2):
                                ins = pe.matmul(o, lhsT=stbf[:, 2 * h + i, j * 128:(j + 1) * 128], rhs=qdec[:, 2 * h + i, :],
                                                start=False, stop=(i == 1))
                    return ins
                sc.add("pe", fnO, reads=[("vt", lp), ("atbf", p), ("qdec", p), ("stbf", 2 * h), ("stbf", 2 * h + 1)], writes=[("bank", bo)])
                act(osq[hp][:].rearrange("p j c -> p (j c)"), banks[bo][:], AF.Square, [("bank", bo)], [("osq", hp)])
                if tt < NT - 1:
                    for i in range(2):
                        dch = 2 * h + i
                        bs = 5 + i
                        mm(banks[bs][:], [(kendT[:, dch * 128:(dch + 1) * 128], vt[lp][:, h * 512:(h + 1) * 512])],
                           [("kendT", p), ("vt", lp)], bs)
                        if tt == 0:
                            dve(lambda e, bs=bs, dch=dch: e.tensor_copy(out=st32[:, dch, :], in_=banks[bs][:]), [("bank", bs)], [("st32", dch)])
                        else:
                            dve(lambda e, bs=bs, dch=dch: e.scalar_tensor_tensor(out=st32[:, dch, :], in0=st32[:, dch, :],
                                                                                  scalar=eb[:, dch, 127:128], in1=banks[bs][:],
                                                                                  op0=ALU.mult, op1=ALU.add),
                                [("bank", bs), ("eb", e3, dch // 4), ("st32", dch)], [("st32", dch)])
                        sc.add("pool", lambda e, dch=dch: e.tensor_copy(out=stbf[:, dch, :], in_=st32[:, dch, :]),
                               reads=[("st32", dch)], writes=[("stbf", dch)])

                def part2():
                    bn = 7
                    mm(banks[bn][:, 0:128], [(ones_512[:], osq[hp][:, j, :]) for j in range(4)], [("osq", hp)], bn)
                    act(rstd[:], banks[bn][:, 0:128], AF.Ln, [("bank", bn)], [("g_rstd", hp)], bias=EPS)
                    act(rstd[:], rstd[:], AF.Exp, [("g_rstd", hp)], [("g_rstd", hp)], scale=-0.5)
                    ra = rstd[:]
                    rb = bass.AP(ra.tensor, ra.offset, [list(ra.ap[0]), [0, 4], list(ra.ap[1])])
                    dve(lambda e: e.tensor_tensor(out=t1[:], in0=banks[bo][:].rearrange("p (j c) -> p j c", j=4), in1=rb, op=ALU.mult),
                        [("bank", bo), ("g_rstd", hp)], [("t1", hp)])
                    dve(lambda e: e.tensor_tensor(out=oT4[og][:, h * 4:(h + 1) * 4, osl], in0=t1[:], in1=r32[lp][:, h * 4:(h + 1) * 4, :], op=ALU.mult),
                        [("t1", hp), ("r32", lp)], [("oT", og, tt % 4, h)])
                    if h == 3 and tt % 4 == 3:
                        g0 = (tt // 4) * 512
                        for hh in range(2):
                            dma("pool", OT[hh * 8:(hh + 1) * 8, :, g0:g0 + 512].rearrange("c p t -> p c t"), oT4[og][:, hh * 8:(hh + 1) * 8, :],
                                reads=[("oT", og, t4, h_) for t4 in range(4) for h_ in range(4)], writes=[("OT", tt // 4, hh)])
                return part2

            loads(0)
            loads(1)
            S0 = front_stages(0)
            for k_ in ("F1", "F2", "F3q", "F3k", "F3b", "F4"):
                S0[k_]()
            if NT > 1:
                S1 = front_stages(1)
                S1["F1"]()
                S1["F2"]()
            deferred = None
            for tt in range(NT):
                Sn = front_stages(tt + 1) if tt + 1 < NT else None
                Sn2 = front_stages(tt + 2) if tt + 2 < NT else None
                for h in range(4):
                    p2 = back_head(tt, h)
                    if deferred is not None:
                        deferred()
                    deferred = p2
                    if h == 0:
                        if Sn:
                            Sn["F3q"]()
                        if tt + 2 < NT:
                            loads(tt + 2)
                    elif h == 1:
                        if Sn:
                            Sn["F3k"]()
                    elif h == 2:
                        if Sn:
                            Sn["F3b"]()
                        if Sn2:
                            Sn2["F1"]()
                    else:
                        if Sn:
                            Sn["F4"]()
                        if Sn2:
                            Sn2["F2"]()
            deferred()
        sc.barrier()

    def phase_ATT():
        with contextlib.ExitStack() as st:
            def t(name, shape, dt=F32):
                return st.enter_context(sbt(name, list(shape), dt))
            oT = t("a_oT", [128, NCH, S], BF16)
            qn = [t("a_q%d" % i, [128, S], BF16) for i in range(2)]
            kn = [t("a_k%d" % i, [128, S], BF16) for i in range(2)]
            vh = [t("a_v%d" % i, [128, 16, 128], BF16) for i in range(2)]
            hk = [t("a_hk%d" % i, [128, 640]) for i in range(2)]
            bt = [t("a_bt%d" % i, [128, 640], BF16) for i in range(2)]
            mB = t("a_mB", [128, 640])
            NR = 8
            pt = [t("a_pt%d" % i, [128, 640], BF16) for i in range(NR)]
            rden = [t("a_rd%d" % i, [128, 512]) for i in range(2)]
            dma("sp", mB[:], cmaskB_d, writes=["mB"])
            NJ = S // 128
            gctr = [0]
            octr = [0]

            def head(h):
                p = h % 2
                dma("sp", qn[p][:], QN[h], writes=[("qn", p)])
                dma("sp", kn[p][:], KN[h], writes=[("kn", p)])
                dma("sp", vh[p][:], VTOK[:, h * 128:(h + 1) * 128].rearrange("(t p) d -> p t d", p=128), writes=[("vh", p)])
                hsrc = bass.AP(extr_t, h * 768, [[1, 128], [128, 5], [1, 128]])
                dma("sp", hk[p][:].rearrange("p (r c) -> p r c", r=5), hsrc, writes=[("hk", p)])
                dve(lambda e: e.tensor_tensor(out=bt[p][:], in0=hk[p][:], in1=mB[:], op=ALU.add), [("hk", p), "mB"], [("bt", p)])
                slot_of = {}

                def qk(kt):
                    g = gctr[0]
                    gctr[0] += 1
                    ps_ = g % NR
                    slot_of[kt] = ps_
                    nq = min(5, NJ - kt)
                    nA = min(nq, 4)
                    bA = 4 + 2 * (g % 2)
                    bB = bA + 1
                    c0 = kt * 128

                    def fnS(pe):
                        pe.matmul(banks[bA][:, 0:nA * 128], lhsT=kn[p][:, c0:c0 + 128], rhs=qn[p][:, c0:c0 + nA * 128], start=True, stop=False)
                        ins = pe.matmul(banks[bA][:, 0:nA * 128], lhsT=anti_bf[:], rhs=bt[p][:, 0:nA * 128], start=False, stop=True)
                        if nq == 5:
                            pe.matmul(banks[bB][:, 0:128], lhsT=kn[p][:, c0:c0 + 128], rhs=qn[p][:, c0 + 512:c0 + 640], start=True, stop=False)
                            ins = pe.matmul(banks[bB][:, 0:128], lhsT=anti_bf[:], rhs=bt[p][:, 512:640], start=False, stop=True)
                        return ins
                    sc.add("pe", fnS, reads=[("kn", p), ("qn", p), ("bt", p)], writes=[("bank", bA), ("bank", bB)])
                    act(pt[ps_][:, 0:nA * 128], banks[bA][:, 0:nA * 128], AF.Exp, [("bank", bA)], [("pt", ps_, 0)])
                    if nq == 5:
                        act(pt[ps_][:, 512:640], banks[bB][:, 0:128], AF.Exp, [("bank", bB)], [("pt", ps_, 1)])

                def pv(j):
                    oi = octr[0] // 4
                    jj = octr[0] % 4
                    octr[0] += 1
                    ob = 2 * (oi % 2)
                    db = ob + 1
                    kts = list(range(max(0, j - 4), j + 1))

                    def fnV(pe):
                        ins = None
                        for i, kt in enumerate(kts):
                            r = j - kt
                            rhs = pt[slot_of[kt]][:, r * 128:(r + 1) * 128]
                            pe.matmul(banks[ob][:, jj * 128:(jj + 1) * 128], lhsT=vh[p][:, kt, :], rhs=rhs, start=(i == 0), stop=(i == len(kts) - 1))
                        for i, kt in enumerate(kts):
                            r = j - kt
                            rhs = pt[slot_of[kt]][:, r * 128:(r + 1) * 128]
                            ins = pe.matmul(banks[db][:, jj * 128:(jj + 1) * 128], lhsT=ones_1[:], rhs=rhs, start=(i == 0), stop=(i == len(kts) - 1))
                        return ins
                    rd = [("pt", slot_of[kt], 1 if (j - kt) == 4 else 0) for kt in kts]
                    sc.add("pe", fnV, reads=[("vh", p)] + rd, writes=[("bank", ob), ("bank", db)])
                    if jj == 3:
                        j0 = j - 3
                        rdn = rden[oi % 2]
                        act(rdn[:], banks[db][:], AF.Ln, [("bank", db)], [("rden", oi % 2)])
                        act(rdn[:], rdn[:], AF.Exp, [("rden", oi % 2)], [("rden", oi % 2)], scale=-1.0)
                        dve(lambda e: e.tensor_tensor(out=oT[:, h, j0 * 128:(j0 + 4) * 128], in0=banks[ob][:], in1=rdn[:], op=ALU.mult),
                            [("bank", ob), ("rden", oi % 2)], [("oT", h, j0)])

                qk(0)
                for kt in range(NJ):
                    if kt + 1 < NJ:
                        qk(kt + 1)
                    pv(kt)

            for h in range(16):
                head(h)
            for hh in range(2):
                dma("sp", OT[hh * 8:(hh + 1) * 8].rearrange("c p t -> p c t"), oT[:, hh * 8:(hh + 1) * 8, :],
                    reads=[("oT", h, j0) for h in range(hh * 8, hh * 8 + 8) for j0 in range(0, NJ, 4)], writes=[("OT", hh)])
        sc.barrier()

    def phase_P3(l, last):
        with contextlib.ExitStack() as st:
            x_fm = st.enter_context(sbt("p3x", [128, NCH, TB], F32))
            a = st.enter_context(sbt("p3a", [128, NCH, TB], BF16))
            u = st.enter_context(sbt("p3u", [128, NCH, TB], BF16))
            sq = [st.enter_context(sbt("p3sq%d" % i, [128, 512], BF16)) for i in range(2)]
            rstd = st.enter_context(sbt("p3rs", [128, TB], F32))
            rl = [st.enter_context(sbt("p3rl%d" % i, [128, 512], F32)) for i in range(2)]
            yst = [st.enter_context(sbt("p3y%d" % i, [128, D], F32)) for i in range(2)] if last else None
            rlc = [0]
            b6 = [0]

            def nb6():
                b6[0] = (b6[0] + 1) % 6
                return b6[0]
            sq4 = sq + [st.enter_context(sbt("p3sq%d" % (i + 2), [128, 512], BF16)) for i in range(2)]
            sqc = [0]
            pend = []

            def flushp():
                while pend:
                    pend.pop(0)()
            for tb in range(NTB):
                t0 = tb * TB
                def load_a(tb_):
                    for cg in range(4):
                        dma("sp", a[:, cg * 4:(cg + 1) * 4, :], OT[cg * 4:(cg + 1) * 4, :, tb_ * TB:(tb_ + 1) * TB].rearrange("c p t -> p c t"),
                            writes=[("act", c, th) for c in range(cg * 4, cg * 4 + 4) for th in range(NTH)])
                if tb == 0:
                    load_a(0)
                for cg in range(4):
                    dma("sp", x_fm[:, cg * 4:(cg + 1) * 4, :], XT[cg * 4:(cg + 1) * 4, :, t0:t0 + TB].rearrange("c p t -> p c t"),
                        writes=[("x", c, th) for c in range(cg * 4, cg * 4 + 4) for th in range(NTH)])
                areads = {th: [("act", k, th) for k in range(NCH)] for th in range(NTH)}
                ureads = {th: [("u", k, th) for k in range(NCH)] for th in range(NTH)}
                for pc in range(8):
                    ws = wnext(("out", l, tb, pc))
                    W = wslots[ws]
                    for cc in range(2):
                        c = pc * 2 + cc
                        for th in range(NTH):
                            ts = slice(th * 512, (th + 1) * 512)
                            b = nb6()
                            mm(banks[b][:], [(W[:, k, cc * 128:(cc + 1) * 128], a[:, k, ts]) for k in range(NCH)], areads[th] + [("w", ws)], b)
                            dve(lambda e, b=b, c=c, ts=ts: e.tensor_tensor(out=x_fm[:, c, ts], in0=x_fm[:, c, ts], in1=banks[b][:], op=ALU.add),
                                [("bank", b), ("x", c, th)], [("x", c, th)])
                            s_ = sqc[0] % 4
                            sqc[0] += 1
                            act(sq4[s_][:], x_fm[:, c, ts], AF.Square, [("x", c, th)], [("sq", s_)])
                            flushp()

                            def ssq(s_=s_, c=c, th=th):
                                sc.add("pe", lambda pe: pe.matmul(banks[6 + th][:], lhsT=ones_d[:], rhs=sq4[s_][:], start=(c == 0), stop=(c == NCH - 1)),
                                       reads=[("sq", s_)], writes=[("bank", 6 + th)])
                            pend.append(ssq)
                flushp()
                for th in range(NTH):
                    ts = slice(th * 512, (th + 1) * 512)
                    act(rstd[:, ts], banks[6 + th][:], AF.Ln, [("bank", 6 + th)], [("rstd", th)], bias=EPS)
                    act(rstd[:, ts], rstd[:, ts], AF.Exp, [("rstd", th)], [("rstd", th)], scale=-0.5)
                    for c in range(NCH):
                        gb = GC_MLP[l] + c
                        dve(lambda e, c=c, ts=ts, gb=gb: e.scalar_tensor_tensor(out=a[:, c, ts], in0=x_fm[:, c, ts], scalar=gcols[:, gb:gb + 1],
                                                                                 in1=rstd[:, ts], op0=ALU.mult, op1=ALU.mult),
                            [("x", c, th), ("rstd", th)], [("act", c, th)])
                for q in range(4):
                    for pc in range(8):
                        ws = wnext(("up", l, tb, q, pc))
                        W = wslots[ws]
                        for cc in range(2):
                            j = pc * 2 + cc
                            for th in range(NTH):
                                ts = slice(th * 512, (th + 1) * 512)
                                b = nb6()
                                mm(banks[b][:], [(W[:, k, cc * 128:(cc + 1) * 128], a[:, k, ts]) for k in range(NCH)], areads[th] + [("w", ws)], b)
                                r_ = rlc[0] % 2
                                rlc[0] += 1
                                act(rl[r_][:], banks[b][:], AF.Relu, [("bank", b)], [("rl", r_)])
                                dve(lambda e, r_=r_, j=j, ts=ts: e.tensor_tensor(out=u[:, j, ts], in0=rl[r_][:], in1=rl[r_][:], op=ALU.mult),
                                    [("rl", r_)], [("u", j, th)])
                    for pc in range(8):
                        ws = wnext(("dn", l, tb, q, pc))
                        W = wslots[ws]
                        for cc in range(2):
                            c = pc * 2 + cc
                            for th in range(NTH):
                                ts = slice(th * 512, (th + 1) * 512)
                                b = nb6()
                                mm(banks[b][:], [(W[:, k, cc * 128:(cc + 1) * 128], u[:, k, ts]) for k in range(NCH)], ureads[th] + [("w", ws)], b)
                                dve(lambda e, b=b, c=c, ts=ts: e.tensor_tensor(out=x_fm[:, c, ts], in0=x_fm[:, c, ts], in1=banks[b][:], op=ALU.add),
                                    [("bank", b), ("x", c, th)], [("x", c, th)])
                if tb + 1 < NTB:
                    load_a(tb + 1)
                if not last:
                    for cg in range(4):
                        dma("sp", XT[cg * 4:(cg + 1) * 4, :, t0:t0 + TB].rearrange("c p t -> p c t"), x_fm[:, cg * 4:(cg + 1) * 4, :],
                            reads=[("x", c, th) for c in range(cg * 4, cg * 4 + 4) for th in range(NTH)], writes=[("XTo", tb, cg)])
                else:
                    for tt in range(TB // 128):
                        th = tt // 4
                        yp = tt % 2
                        for cq in range(4):
                            b = nb6()

                            def fnY(pe, b=b, cq=cq, tt=tt):
                                ins = None
                                for i in range(4):
                                    c = cq * 4 + i
                                    ins = pe.transpose(out=banks[b][:, i * 128:(i + 1) * 128], in_=x_fm[:, c, tt * 128:(tt + 1) * 128], identity=ident[:])
                                return ins
                            sc.add("pe", fnY, reads=[("x", cq * 4 + i, th) for i in range(4)], writes=[("bank", b)])
                            if cq % 2 == 0:
                                act(yst[yp][:, cq * 512:(cq + 1) * 512], banks[b][:], AF.Copy, [("bank", b)], [("yst", yp, cq)])
                            else:
                                dve(lambda e, b=b, yp=yp, cq=cq: e.tensor_copy(out=yst[yp][:, cq * 512:(cq + 1) * 512], in_=banks[b][:]),
                                    [("bank", b)], [("yst", yp, cq)])
                        r0 = t0 + tt * 128
                        dma("sp", y_out[r0:r0 + 128, :], yst[yp][:], reads=[("yst", yp, cq) for cq in range(4)], writes=[("y", r0)])
        sc.barrier()

    phases = [phase_A, lambda: phase_P1(0), phase_GLA, lambda: phase_P3(0, False),
              lambda: phase_P1(1), phase_ATT, lambda: phase_P3(1, True)]
    if only == "gla":
        phases = [phase_GLA]
    if only == "att":
        phases = [phase_ATT]
    for i, ph in enumerate(phases):
        if i >= stop_after:
            break
        ph()
    sc.emit(nc, stack)
    stack.close()
    return nc


def host_consts():
    c = {}
    c["c_ident"] = np.eye(128, dtype=np.float32)
    c["c_anti"] = np.ascontiguousarray(np.eye(128, dtype=np.float32)[::-1])
    s = np.arange(128)[:, None]
    t = np.arange(128)[None, :]
    c["c_ltri"] = np.where(s <= t, np.float32(-1.0 / 16.0), np.float32(0.0)).astype(np.float32)
    c["c_maskA"] = np.tile((s <= t).astype(np.float32), (1, 4))
    k = np.arange(128)[:, None, None]
    r = np.arange(5)[None, :, None]
    cq = np.arange(128)[None, None, :]
    m = 127 - k
    diff = 2 * r + (cq >= 64).astype(np.int64) - (m >= 64).astype(np.int64)
    c["c_maskB"] = np.where((diff >= 0) & (diff <= 8), np.float32(0.0), np.float32(NEG)).astype(np.float32).reshape(128, 640)
    c["c_onesrow"] = np.ones((1, S), dtype=np.float32)
    return c


_CACHE = {}


def kernel(x, norm_mix_g, norm_mlp_g, gla_w_in, gla_w_gate_up, gla_b_gate, gla_g_out, gla_w_out,
           att_w_in, att_g_q, att_g_k, att_rel_bias, att_w_out, mlp_w_up, mlp_w_down):
    f = lambda a: np.ascontiguousarray(np.asarray(a, dtype=np.float32))
    x = f(x)
    def cols(v):
        return np.ascontiguousarray(f(v).reshape(-1, 128).T)
    nm = f(norm_mix_g)
    nl = f(norm_mlp_g)
    gcols = np.concatenate([cols(nm[0]), cols(nl[0]), cols(nm[1]), cols(nl[1]), cols(f(gla_g_out)[0]),
                            cols(f(att_g_q)[0]), cols(f(att_g_k)[0])], axis=1)
    assert gcols.shape == (128, 82)
    idx = np.clip(np.arange(768) - 127, -63, 256) + 63
    ext = np.ascontiguousarray(f(att_rel_bias)[0][:, idx])
    shared = {
        "gla_w_in": f(gla_w_in)[0], "gla_w_gate_up": f(gla_w_gate_up)[0], "gla_b_gate": f(gla_b_gate).reshape(1, 1024),
        "gla_w_out": f(gla_w_out)[0], "att_w_in": f(att_w_in)[0], "att_w_out": f(att_w_out)[0],
        "mlp_w_up": f(mlp_w_up), "mlp_w_down": f(mlp_w_down), "att_ext": ext, "gcols": np.ascontiguousarray(gcols),
    }
    shared.update(host_consts())
    if "nc" not in _CACHE:
        _CACHE["nc"] = build()
    nc = _CACHE["nc"]
    in_maps = []
    for b in range(8):
        m = dict(shared)
        m["x"] = x[b]
        in_maps.append(m)
    res = run_bass_kernel_spmd(nc, in_maps, core_ids=list(range(8)))
    return np.stack([np.asarray(r["y"], dtype=np.float32) for r in res.results], axis=0)
```

```python
import contextlib
import numpy as np
import concourse.bass as bass
import concourse.mybir as mybir
from concourse.bass_utils import run_bass_kernel_spmd

F32 = mybir.dt.float32
BF16 = mybir.dt.bfloat16
AF = mybir.ActivationFunctionType
ALU = mybir.AluOpType

S = 2048
D = 2048
NCH = 16
TB = 1024
NTB = S // TB
NTH = TB // 512
DFF = 8192
EPS = 1e-6
ROLL = 30000
NSLOT = 8
NWSLOT = 4
GLA_COLS = 6160
NEG = -30000.0
LN_QS = float(np.log(256.0 ** -0.5))


class Op:
    __slots__ = ("eng", "fn", "waits", "dwaits", "isdma", "sig", "sigcount", "slot", "slotval", "idx", "vc", "vd", "k")


class Sched:
    ENGS = ("pe", "act", "dve", "pool", "sp")

    def __init__(self):
        self.ops = {e: [] for e in self.ENGS}
        self.lastw = {}
        self.readers = {}
        self.known = {e: {} for e in self.ENGS}
        self.knownd = {e: {} for e in self.ENGS}
        self.ndma = {"sp": 0, "pool": 0}
        self.lastc = {}
        self.lastd = {}

    def _finish(self, op, deps):
        eng = op.eng
        kn = self.known[eng]
        kd = self.knownd[eng]
        waits = {}
        dwaits = {}
        for d in deps:
            if d is op:
                continue
            if d.isdma:
                key = (d.eng, d.slot)
                if kd.get(key, 0) < d.slotval:
                    kd[key] = d.slotval
                    dwaits[key] = max(dwaits.get(key, 0), d.slotval)
                    self._merge(kn, kd, d)
            else:
                if d.eng == "pe" and eng == "pe":
                    continue
                if kn.get(d.eng, -1) < d.idx:
                    kn[d.eng] = d.idx
                    old = waits.get(d.eng)
                    if old is None or old.idx < d.idx:
                        waits[d.eng] = d
                    self._merge(kn, kd, d)
        op.waits = waits
        op.dwaits = dwaits
        op.vc = dict(kn)
        op.vd = dict(kd)
        op.sig = False
        op.sigcount = 0
        op.idx = len(self.ops[eng])
        self.ops[eng].append(op)

    @staticmethod
    def _merge(kn, kd, d):
        for f, i in d.vc.items():
            if kn.get(f, -1) < i:
                kn[f] = i
        for f, v in d.vd.items():
            if kd.get(f, 0) < v:
                kd[f] = v

    def add(self, eng, fn, reads=(), writes=(), dma=False):
        op = Op()
        op.eng = eng
        op.fn = fn
        op.isdma = dma
        op.slot = 0
        op.slotval = 0
        op.k = 0
        if dma:
            k = self.ndma[eng]
            self.ndma[eng] = k + 1
            op.k = k
            op.slot = k % NSLOT
            op.slotval = 16 * (k // NSLOT + 1)
        deps = []
        for r in reads:
            w = self.lastw.get(r)
            if w is not None:
                deps.append(w)
        for x in writes:
            w = self.lastw.get(x)
            if w is not None:
                deps.append(w)
            deps.extend(self.readers.get(x, ()))
        self._finish(op, deps)
        for r in reads:
            self.readers.setdefault(r, []).append(op)
        for x in writes:
            self.lastw[x] = op
            self.readers[x] = []
        if dma:
            self.lastd[(eng, op.slot)] = op
        else:
            self.lastc[eng] = op
        return op

    def barrier(self):
        deps = list(self.lastc.values()) + list(self.lastd.values())
        for e in self.ENGS:
            op = Op()
            op.eng = e
            op.fn = None
            op.isdma = False
            op.slot = 0
            op.slotval = 0
            op.k = 0
            self._finish(op, deps)
        self.lastw.clear()
        self.readers.clear()

    def emit(self, nc, stack):
        for e in self.ENGS:
            for op in self.ops[e]:
                for d in op.waits.values():
                    d.sig = True
        nsig = {}
        for e in self.ENGS:
            c = 0
            for op in self.ops[e]:
                if op.sig:
                    c += 1
                    op.sigcount = c
            nsig[e] = c
        engsem = {}
        for e in self.ENGS:
            n = max(1, (nsig[e] + ROLL - 1) // ROLL)
            engsem[e] = [stack.enter_context(nc.semaphore("s_%s_%d" % (e, i))) for i in range(n)]
        dmasem = {q: [stack.enter_context(nc.semaphore("d_%s_%d" % (q, i))) for i in range(NSLOT)] for q in ("sp", "pool")}
        block = stack.enter_context(nc.Block())
        ndma = self.ndma

        def run(name, eng):
            for op in self.ops[name]:
                for f, d in op.waits.items():
                    k, v = divmod(d.sigcount - 1, ROLL)
                    eng.wait_ge(engsem[f][k], v + 1)
                for (q, slot), val in op.dwaits.items():
                    eng.wait_ge(dmasem[q][slot], val)
                if op.fn is None:
                    continue
                if op.isdma:
                    if op.k >= NSLOT:
                        eng.wait_ge(dmasem[name][op.slot], op.slotval - 16)
                    ins = op.fn(eng)
                    ins.then_inc(dmasem[name][op.slot], 16)
                else:
                    ins = op.fn(eng)
                    if op.sig:
                        ins.then_inc(engsem[name][(op.sigcount - 1) // ROLL], 1)
            if name == "sp":
                for q in ("sp", "pool"):
                    for slot in range(NSLOT):
                        n = (ndma[q] - slot + NSLOT - 1) // NSLOT
                        if n > 0:
                            eng.wait_ge(dmasem[q][slot], 16 * n)

        @block.tensor
        def _(e):
            run("pe", e)

        @block.scalar
        def _(e):
            run("act", e)

        @block.vector
        def _(e):
            run("dve", e)

        @block.gpsimd
        def _(e):
            run("pool", e)

        @block.sync
        def _(e):
            run("sp", e)


def build(debug=False, stop_after=99, only=None, opts=()):
    nc = bass.Bass("TRN2", target_bir_lowering=False)
    sc = Sched()
    stack = contextlib.ExitStack()
    uctr = [0]

    def sbt(name, shape, dt):
        uctr[0] += 1
        return nc.sbuf_tensor("%s_u%d" % (name, uctr[0]), shape, dt)

    def din(name, shape, dt=F32):
        if only is not None and int(np.prod(shape)) > (1 << 20):
            shape = [1] * (len(shape) - 1) + [16]
        return nc.dram_tensor(name, list(shape), dt, kind="ExternalInput")

    skind = "ExternalOutput" if debug else "Internal"

    def dscr(name, shape, dt=F32):
        kind = skind
        if only == "gla" and name in ("QT", "KT", "RT", "ZT", "VTOK"):
            kind = "ExternalInput"
        if only == "att" and name in ("QN", "KN", "VTOK"):
            kind = "ExternalInput"
        return nc.dram_tensor(name, list(shape), dt, kind=kind)

    x_in = din("x", [S, D]).ap()
    y_out = nc.dram_tensor("y", [S, D], F32, kind="ExternalOutput").ap()
    gla_w_in = din("gla_w_in", [D, GLA_COLS]).ap()
    gla_w_gate = din("gla_w_gate_up", [16, 1024]).ap()
    gla_b_gate = din("gla_b_gate", [1, 1024]).ap()
    gla_w_out = din("gla_w_out", [D, D]).ap()
    att_w_in = din("att_w_in", [D, 3 * D]).ap()
    att_w_out = din("att_w_out", [D, D]).ap()
    mlp_w_up = din("mlp_w_up", [2, D, DFF]).ap()
    mlp_w_down = din("mlp_w_down", [2, DFF, D]).ap()
    extr_t = din("att_ext", [16, 768])
    gcols_d = din("gcols", [128, 82]).ap()
    ident_d = din("c_ident", [128, 128]).ap()
    anti_d = din("c_anti", [128, 128]).ap()
    ltri_d = din("c_ltri", [128, 128]).ap()
    cmaskA_d = din("c_maskA", [128, 512]).ap()
    cmaskB_d = din("c_maskB", [128, 640]).ap()
    onesrow_d = din("c_onesrow", [1, S]).ap()

    XT = dscr("XT", [NCH, 128, S]).ap()
    QT = dscr("QT", [16, 128, 8, 128], BF16).ap()
    KT = dscr("KT", [16, 128, 8, 128], BF16).ap()
    RT = dscr("RT", [16, 128, NCH, 128], BF16).ap()
    ZT = dscr("ZT", [16, S]).ap()
    VTOK = dscr("VTOK", [S, D], BF16).ap()
    OT = dscr("OT", [NCH, 128, S], BF16).ap()
    QN = dscr("QN", [NCH, 128, S], BF16).ap()
    KN = dscr("KN", [NCH, 128, S], BF16).ap()

    def sb(name, shape, dt=F32):
        return stack.enter_context(sbt("sb_" + name, list(shape), dt))

    banks = [stack.enter_context(nc.psum_tensor("ps%d" % i, [128, 512], F32)) for i in range(8)]
    bank_ctr = [0]

    def nb():
        b = bank_ctr[0] % 8
        bank_ctr[0] += 1
        return b

    wslots = [sb("w%d" % i, [128, 16, 256], BF16) for i in range(NWSLOT)]
    gcols = sb("gcols", [128, 82])
    ident = sb("ident", [128, 128])
    ident_bf = sb("ident_bf", [128, 128], BF16)
    anti_bf = sb("anti_bf", [128, 128], BF16)
    ones_d = sb("ones_d", [128, 128], BF16)
    ones_512 = sb("ones_512", [128, 128], BF16)
    ones_128 = sb("ones_128", [128, 128], BF16)
    ones_1 = sb("ones_1", [128, 128], BF16)
    tmpc = sb("tmpc", [128, 128])
    gq_s = sb("gq_s", [128, 1])
    epsc = sb("epsc", [128, 1])

    wlist = []
    wstate = {"issued": 0, "used": 0}

    def wreg(key, ap, ncols=256):
        wlist.append((key, ap, ncols))

    def w_issue_upto(n):
        while wstate["issued"] < min(n, len(wlist)):
            i = wstate["issued"]
            key, ap, ncols = wlist[i]
            slot = i % NWSLOT
            dst = wslots[slot][:, :, 0:ncols]
            src = ap.rearrange("(k p) m -> p k m", p=128)
            sc.add("pool", lambda e, dst=dst, src=src: e.dma_start(out=dst, in_=src),
                   writes=[("w", slot)], dma=True)
            wstate["issued"] += 1

    def wnext(key):
        i = wstate["used"]
        assert wlist[i][0] == key, (wlist[i][0], key)
        w_issue_upto(i + NWSLOT)
        wstate["used"] += 1
        return i % NWSLOT

    for l in range(2 if only is None else 0):
        w_in = gla_w_in if l == 0 else att_w_in
        npc = 24
        for tb in range(NTB):
            for pc in range(npc):
                wreg(("in", l, tb, pc), w_in[:, pc * 256:(pc + 1) * 256])
            if l == 0:
                wreg(("in", l, tb, 24), w_in[:, 6144:6160], 16)
        w_o = gla_w_out if l == 0 else att_w_out
        for tb in range(NTB):
            for pc in range(8):
                wreg(("out", l, tb, pc), w_o[:, pc * 256:(pc + 1) * 256])
            for q in range(4):
                for pc in range(8):
                    c0 = q * 2048 + pc * 256
                    wreg(("up", l, tb, q, pc), mlp_w_up[l, :, c0:c0 + 256])
                for pc in range(8):
                    wreg(("dn", l, tb, q, pc), mlp_w_down[l, q * 2048:(q + 1) * 2048, pc * 256:(pc + 1) * 256])

    def mm(out, pairs, reads, bank, extra_writes=()):
        def fn(pe, out=out, pairs=pairs):
            n = len(pairs)
            ins = None
            for i, (l, r) in enumerate(pairs):
                ins = pe.matmul(out, lhsT=l, rhs=r, start=(i == 0), stop=(i == n - 1))
            return ins
        return sc.add("pe", fn, reads=reads, writes=[("bank", bank)] + list(extra_writes))

    def dma(q, out, in_, reads=(), writes=()):
        return sc.add(q, lambda e, out=out, in_=in_: e.dma_start(out=out, in_=in_), reads=reads, writes=writes, dma=True)

    def act(out, in_, func, reads, writes, bias=None, scale=None):
        def fn(e, out=out, in_=in_, func=func, bias=bias, scale=scale):
            kw = {}
            if bias is not None:
                kw["bias"] = bias
            if scale is not None:
                kw["scale"] = scale
            return e.activation(out=out, in_=in_, func=func, **kw)
        return sc.add("act", fn, reads=reads, writes=writes)

    def dve(fn, reads, writes):
        return sc.add("dve", fn, reads=reads, writes=writes)

    dma("sp", gcols[:], gcols_d, writes=["gcols"])
    dma("sp", ident[:], ident_d, writes=["ident"])
    dma("sp", tmpc[:], anti_d, writes=["tmpc"])
    dve(lambda e: e.tensor_copy(out=ident_bf[:], in_=ident[:]), ["ident"], ["ident_bf"])
    dve(lambda e: e.tensor_copy(out=anti_bf[:], in_=tmpc[:]), ["tmpc"], ["anti_bf"])
    dve(lambda e: e.memset(ones_d[:], 1.0 / D), [], ["ones_d"])
    dve(lambda e: e.memset(ones_512[:], 1.0 / 512), [], ["ones_512"])
    dve(lambda e: e.memset(ones_128[:], 1.0 / 128), [], ["ones_128"])
    dve(lambda e: e.memset(ones_1[:], 1.0), [], ["ones_1"])
    dve(lambda e: e.memset(epsc[:], EPS), [], ["epsc"])
    dve(lambda e: e.tensor_scalar(out=gq_s[:], in0=gcols[:, 80:81], scalar1=128.0 ** -0.5, scalar2=None, op0=ALU.mult),
        ["gcols"], ["gq_s"])
    sc.barrier()

    GC_MIX = [0, 32]
    GC_MLP = [16, 48]
    GC_GOUT = 64
    GC_GK = 81

    def phase_A():
        with contextlib.ExitStack() as st:
            xt8 = [st.enter_context(sbt("xa%d" % i, [128, D], F32)) for i in range(8)]
            stg = [st.enter_context(sbt("xs%d" % i, [128, 8, 512], F32)) for i in range(2)]
            for g in range(S // 512):
                xt = xt8[(g % 2) * 4:(g % 2) * 4 + 4]
                for i in range(4):
                    r0 = g * 512 + i * 128
                    dma("sp", xt[i][:], x_in[r0:r0 + 128, :], writes=[("xa", g % 2, i)])
                for hh in range(2):
                    sg = stg[hh]
                    for c8 in range(8):
                        c = hh * 8 + c8
                        b = nb()

                        def fn(pe, b=b, c=c, xt=xt):
                            ins = None
                            for i in range(4):
                                ins = pe.transpose(out=banks[b][:, i * 128:(i + 1) * 128],
                                                   in_=xt[i][:, c * 128:(c + 1) * 128], identity=ident[:])
                            return ins
                        sc.add("pe", fn, reads=[("xa", g % 2, i) for i in range(4)], writes=[("bank", b)])
                        if c % 2 == 0:
                            act(sg[:, c8, :], banks[b][:], AF.Copy, [("bank", b)], [("xs", hh, c8)])
                        else:
                            dve(lambda e, b=b, sg=sg, c8=c8: e.tensor_copy(out=sg[:, c8, :], in_=banks[b][:]),
                                [("bank", b)], [("xs", hh, c8)])
                    dma("pool", XT[hh * 8:(hh + 1) * 8, :, g * 512:(g + 1) * 512].rearrange("c p t -> p c t"), sg[:],
                        reads=[("xs", hh, c8) for c8 in range(8)], writes=[("XT", g, hh)])
        sc.barrier()

    def norm_fm(x_fm, hT, gbase, sq, rstd):
        for th in range(NTH):
            ts = slice(th * 512, (th + 1) * 512)
            b = nb()
            for c in range(NCH):
                s_ = c % 2
                act(sq[s_][:], x_fm[:, c, ts], AF.Square, [("x", c, th)], [("sq", s_)])
                sc.add("pe", lambda pe, b=b, s_=s_, c=c: pe.matmul(banks[b][:], lhsT=ones_d[:], rhs=sq[s_][:],
                                                                     start=(c == 0), stop=(c == NCH - 1)),
                       reads=[("sq", s_)], writes=[("bank", b)])
            act(rstd[:, ts], banks[b][:], AF.Ln, [("bank", b)], [("rstd", th)], bias=EPS)
            act(rstd[:, ts], rstd[:, ts], AF.Exp, [("rstd", th)], [("rstd", th)], scale=-0.5)
            for c in range(NCH):
                dve(lambda e, c=c, ts=ts: e.scalar_tensor_tensor(out=hT[:, c, ts], in0=x_fm[:, c, ts],
                                                                  scalar=gcols[:, gbase + c:gbase + c + 1],
                                                                  in1=rstd[:, ts], op0=ALU.mult, op1=ALU.mult),
                    [("x", c, th), ("rstd", th)], [("act", c, th)])

    def phase_P1(l):
        with contextlib.ExitStack() as st:
            x_fm = st.enter_context(sbt("p1x", [128, NCH, TB], F32))
            hT = st.enter_context(sbt("p1h", [128, NCH, TB], BF16))
            vst = st.enter_context(sbt("p1v", [128, 8, D], BF16))
            sq = [st.enter_context(sbt("p1sq%d" % i, [128, 512], BF16)) for i in range(2)]
            rstd = st.enter_context(sbt("p1rs", [128, TB], F32))
            stg = [st.enter_context(sbt("p1st%d" % i, [128, 512], F32)) for i in range(4)]
            stb = [st.enter_context(sbt("p1sb%d" % i, [128, 512], BF16)) for i in range(4)]
            rs2 = [st.enter_context(sbt("p1r2%d" % i, [128, 512], F32)) for i in range(2)]
            stc = [0]
            stp = [0]
            stb2 = [st.enter_context(sbt("p1s2%d" % i, [128, 2, 512], BF16)) for i in range(2)]
            pending = []

            def flush():
                while pending:
                    pending.pop(0)()
            def load_x(tb_):
                for cg in range(4):
                    dma("sp", x_fm[:, cg * 4:(cg + 1) * 4, :], XT[cg * 4:(cg + 1) * 4, :, tb_ * TB:(tb_ + 1) * TB].rearrange("c p t -> p c t"),
                        writes=[("x", c, th) for c in range(cg * 4, cg * 4 + 4) for th in range(NTH)])
            load_x(0)
            for tb in range(NTB):
                t0 = tb * TB
                norm_fm(x_fm, hT, GC_MIX[l], sq, rstd)
                if tb + 1 < NTB:
                    load_x(tb + 1)
                hreads = {th: [("act", k, th) for k in range(NCH)] for th in range(NTH)}
                allh = hreads[0] + hreads[1]
                npc = 25 if l == 0 else 24
                for pc in range(npc):
                    ws = wnext(("in", l, tb, pc))
                    W = wslots[ws]
                    if l == 0:
                        kind = "q" if pc < 4 else "k" if pc < 8 else "v" if pc < 16 else "r" if pc < 24 else "z"
                    else:
                        kind = "q" if pc < 8 else "k" if pc < 16 else "v"
                    if kind == "v":
                        flush()
                        g = pc - (8 if l == 0 else 16)
                        for tt in range(8):
                            b = nb()
                            mm(banks[b][:, 0:256], [(hT[:, k, tt * 128:(tt + 1) * 128], W[:, k, :]) for k in range(NCH)],
                               allh + [("w", ws)], b)
                            if tt % 2 == 0:
                                act(vst[:, tt, g * 256:(g + 1) * 256], banks[b][:, 0:256], AF.Copy, [("bank", b)], [("vst", tt, g)])
                            else:
                                dve(lambda e, b=b, tt=tt, g=g: e.tensor_copy(out=vst[:, tt, g * 256:(g + 1) * 256], in_=banks[b][:, 0:256]),
                                    [("bank", b)], [("vst", tt, g)])
                        if g == 7:
                            dma("sp", VTOK[t0:t0 + TB, :].rearrange("(t p) d -> p t d", p=128), vst[:],
                                reads=[("vst", tt, gg) for tt in range(8) for gg in range(8)], writes=[("VTOK", tb)])
                        continue
                    if kind == "z":
                        for th in range(NTH):
                            ts = slice(th * 512, (th + 1) * 512)
                            b = nb()
                            mm(banks[b][0:16, :], [(W[:, k, 0:16], hT[:, k, ts]) for k in range(NCH)], hreads[th] + [("w", ws)], b)
                            s_ = stc[0] % 4
                            stc[0] += 1
                            act(stg[s_][0:16, :], banks[b][0:16, :], AF.Copy, [("bank", b)], [("stg", s_)])
                            dma("sp", ZT[:, t0 + th * 512:t0 + (th + 1) * 512], stg[s_][0:16, :], reads=[("stg", s_)], writes=[("ZT", tb, th)])
                        continue
                    for th, cc in ([(th, cc) for th in range(NTH) for cc in range(2)] if l == 0 else [(th, cc) for cc in range(2) for th in range(NTH)]):
                        if True:
                            ts = slice(th * 512, (th + 1) * 512)
                            tsl = slice(t0 + th * 512, t0 + (th + 1) * 512)
                            b = nb()
                            mm(banks[b][:], [(W[:, k, cc * 128:(cc + 1) * 128], hT[:, k, ts]) for k in range(NCH)],
                               hreads[th] + [("w", ws)], b)
                            s_ = stc[0] % 4
                            stc[0] += 1
                            if l == 0:
                                base = {"q": 0, "k": 4, "r": 16}[kind]
                                ch = (pc - base) * 2 + cc
                                dst = {"q": QT, "k": KT, "r": RT}[kind]
                                if cc == 0:
                                    stp[0] = (stp[0] + 1) % 2
                                q_ = stp[0]
                                so = stb2[q_][:, cc, :]
                                if kind == "r":
                                    act(stg[s_][:], banks[b][:], AF.Silu, [("bank", b)], [("stg", s_)])
                                    gc = GC_GOUT + ch
                                    act(so, stg[s_][:], AF.Copy, [("stg", s_)], [("stb2", q_, cc)], scale=gcols[:, gc:gc + 1])
                                elif stc[0] % 2 == 0:
                                    act(so, banks[b][:], AF.Copy, [("bank", b)], [("stb2", q_, cc)])
                                else:
                                    dve(lambda e, b=b, so=so: e.tensor_copy(out=so, in_=banks[b][:]), [("bank", b)], [("stb2", q_, cc)])
                                if cc == 1:
                                    tile0 = (t0 + th * 512) // 128
                                    for k4 in range(4):
                                        dma("sp", dst[tile0 + k4, :, ch - 1:ch + 1, :], stb2[q_][:, :, k4 * 128:(k4 + 1) * 128],
                                            reads=[("stb2", q_, 0), ("stb2", q_, 1)], writes=[(kind, ch, tb, th, k4)])
                            else:
                                base = {"q": 0, "k": 8}[kind]
                                hd = (pc - base) * 2 + cc
                                dst = QN if kind == "q" else KN
                                gcol = gq_s[:, 0:1] if kind == "q" else gcols[:, GC_GK:GC_GK + 1]
                                q_ = stc[0] % 2
                                act(sq[q_][:], banks[b][:], AF.Square, [("bank", b)], [("sq", q_)])
                                flush()

                                def epi(b=b, q_=q_, gcol=gcol, dst=dst, hd=hd, tsl=tsl, kind=kind, th=th):
                                    b2 = nb()
                                    rs = rs2[q_]
                                    mm(banks[b2][:], [(ones_128[:], sq[q_][:])], [("sq", q_)], b2)
                                    act(rs[:], banks[b2][:], AF.Ln, [("bank", b2)], [("rs2", q_)], bias=EPS)
                                    act(rs[:], rs[:], AF.Exp, [("rs2", q_)], [("rs2", q_)], scale=-0.5)
                                    dve(lambda e: e.scalar_tensor_tensor(out=stb[q_][:], in0=banks[b][:], scalar=gcol,
                                                                         in1=rs[:], op0=ALU.mult, op1=ALU.mult),
                                        [("bank", b), ("rs2", q_)], [("stb", q_)])
                                    dma("sp", dst[hd, :, tsl], stb[q_][:], reads=[("stb", q_)], writes=[(kind, hd, tb, th)])
                                pending.append(epi)
                flush()
        sc.barrier()

    def phase_GLA():
        with contextlib.ExitStack() as st:
            def t(name, shape, dt=F32):
                return st.enter_context(sbt(name, list(shape), dt))

            def t2(name, shape, dt=F32):
                return [t("%s%d" % (name, i), shape, dt) for i in range(2)]
            oT4 = t2("g_oT", [128, NCH, 512], BF16)
            st32 = t("g_st32", [128, 8, 512])
            stbf = t("g_stbf", [128, 8, 512], BF16)
            zt = [t("g_zT%d" % i, [17, 128]) for i in range(3)]
            wg = t("g_wg", [17, 1024])
            ltri = t("g_ltri", [128, 128])
            mA = t("g_mA", [128, 512])
            sp32s = [t("g_sp%d" % i, [128, 1024]) for i in range(3)]
            ebs = [t("g_eb%d" % i, [128, 8, 128]) for i in range(3)]
            enbs = [t("g_enb%d" % i, [128, 8, 128]) for i in range(3)]
            q32 = [t("g_q%d" % i, [128, 8, 128], BF16) for i in range(3)]
            k32 = [t("g_k%d" % i, [128, 8, 128], BF16) for i in range(3)]
            vt = [t("g_v%d" % i, [128, D], BF16) for i in range(3)]
            r32 = [t("g_r%d" % i, [128, NCH, 128], BF16) for i in range(3)]
            qdecs = t2("g_qd", [128, 8, 128], BF16)
            kinvs = t2("g_ki", [128, 8, 128], BF16)
            tks = t2("g_tk", [128, 8, 128])
            kends = t2("g_ke", [128, 8, 128], BF16)
            kendTs = t2("g_keT", [128, 1024], BF16)
            atbfs = t2("g_at", [128, 4, 128], BF16)
            osq = t2("g_os", [128, 4, 128], BF16)
            rstds = t2("g_rs", [128, 128])
            t1s = t2("g_t1", [128, 4, 128], BF16)

            for i in range(3):
                dma("sp", zt[i][16:17, :], onesrow_d[:, 0:128], writes=[("zT1", i)])
            dma("sp", wg[0:16, :], gla_w_gate, writes=["wg"])
            dma("sp", wg[16:17, :], gla_b_gate, writes=["wg1"])
            dma("sp", ltri[:], ltri_d, writes=["ltri"])
            dma("sp", mA[:], cmaskA_d, writes=["mA"])
            NT = S // 128
            hc = [0]

            def loads(tt):
                tsl = slice(tt * 128, (tt + 1) * 128)
                lp = tt % 3
                dma("sp", zt[lp][0:16, :], ZT[:, tsl], writes=[("zT", lp)])
                dma("sp", q32[lp][:], QT[tt], writes=[("q32", lp)])
                dma("sp", k32[lp][:], KT[tt], writes=[("k32", lp)])
                dma("sp", vt[lp][:], VTOK[tsl, :], writes=[("vt", lp)])
                dma("sp", r32[lp][:], RT[tt], writes=[("r32", lp)])

            def front_stages(tt):
                p = tt % 2
                lp = tt % 3
                e3 = tt % 3
                sp32, eb, enb, qdec, kinv, kend, kendT, atbf = sp32s[e3], ebs[e3], enbs[e3], qdecs[p], kinvs[p], kends[p], kendTs[p], atbfs[p]

                def F1():
                    bl = [0, 1]
                    for hf in range(2):
                        hsl = slice(hf * 512, (hf + 1) * 512)
                        mm(banks[bl[hf]][:], [(zt[lp][0:17, :], wg[0:17, hsl])], [("zT", lp), ("zT1", lp), "wg", "wg1"], bl[hf])
                        act(sp32[:, hsl], banks[bl[hf]][:], AF.Exp, [("bank", bl[hf])], [("sp", e3, hf)], scale=-1.0)
                        act(sp32[:, hsl], sp32[:, hsl], AF.Ln, [("sp", e3, hf)], [("sp", e3, hf)], bias=1.0)

                def F2():
                    bb = [0, 1]
                    for hf in range(2):
                        def fn(pe, hf=hf, b=bb[hf]):
                            ins = None
                            for j in range(4):
                                dch = hf * 4 + j
                                ins = pe.matmul(banks[b][:, j * 128:(j + 1) * 128], lhsT=sp32[:, dch * 128:(dch + 1) * 128], rhs=ltri[:],
                                                start=True, stop=True)
                            return ins
                        sc.add("pe", fn, reads=[("sp", e3, hf), "ltri"], writes=[("bank", bb[hf])])
                        bv = banks[bb[hf]][:].rearrange("p (j t) -> p j t", j=4)
                        hs = slice(hf * 4, (hf + 1) * 4)
                        act(eb[:, hs, :], bv, AF.Exp, [("bank", bb[hf])], [("eb", e3, hf)])
                        act(enb[:, hs, :], bv, AF.Exp, [("bank", bb[hf])], [("enb", e3, hf)], scale=-1.0)

                def F3q():
                    dve(lambda e: e.scalar_tensor_tensor(out=qdec[:], in0=q32[lp][:], scalar=0.0625, in1=eb[:], op0=ALU.mult, op1=ALU.mult),
                        [("q32", lp), ("eb", e3, 0), ("eb", e3, 1)], [("qdec", p)])

                def F3k():
                    dve(lambda e: e.tensor_tensor(out=kinv[:], in0=k32[lp][:], in1=enb[:], op=ALU.mult),
                        [("k32", lp), ("enb", e3, 0), ("enb", e3, 1)], [("kinv", p)])

                def F3b():
                    tk = tks[p]
                    for dch in range(8):
                        act(tk[:, dch, :], k32[lp][:, dch, :], AF.Copy, [("k32", lp), ("eb", e3, dch // 4)], [("tk", p, dch)],
                            scale=eb[:, dch, 127:128])
                    dve(lambda e: e.tensor_tensor(out=kend[:], in0=tk[:], in1=enb[:], op=ALU.mult),
                        [("tk", p, d_) for d_ in range(8)] + [("enb", e3, 0), ("enb", e3, 1)], [("kend", p)])

                def F4():
                    bt = 2

                    def fnT(pe):
                        ins = None
                        o = banks[bt][:].bitcast(BF16)
                        for dch in range(8):
                            ins = pe.transpose(out=o[:, dch * 128:(dch + 1) * 128], in_=kend[:, dch, :], identity=ident_bf[:])
                        return ins
                    sc.add("pe", fnT, reads=[("kend", p)], writes=[("bank", bt)])
                    act(kendT[:], banks[bt][:].bitcast(BF16), AF.Copy, [("bank", bt)], [("kendT", p)])
                    ba = 2

                    def fnA(pe):
                        ins = None
                        for h in range(4):
                            for i in range(2):
                                ins = pe.matmul(banks[ba][:, h * 128:(h + 1) * 128], lhsT=kinv[:, 2 * h + i, :], rhs=qdec[:, 2 * h + i, :],
                                                start=(i == 0), stop=(i == 1))
                        return ins
                    sc.add("pe", fnA, reads=[("kinv", p), ("qdec", p)], writes=[("bank", ba)])
                    dve(lambda e: e.tensor_tensor(out=atbf[:].rearrange("p h c -> p (h c)"), in0=banks[ba][:], in1=mA[:], op=ALU.mult),
                        [("bank", ba), "mA"], [("atbf", p)])
                return dict(F1=F1, F2=F2, F3q=F3q, F3k=F3k, F3b=F3b, F4=F4)

            def back_head(tt, h):
                tsl = slice(tt * 128, (tt + 1) * 128)
                p = tt % 2
                lp = tt % 3
                e3 = tt % 3
                eb, qdec, kendT, atbf = ebs[e3], qdecs[p], kendTs[p], atbfs[p]
                og = (tt // 4) % 2
                osl = slice((tt % 4) * 128, (tt % 4 + 1) * 128)
                hp = hc[0] % 2
                hc[0] += 1
                rstd, t1 = rstds[hp], t1s[hp]
                bo = 3 + hp

                def fnO(pe):
                    ins = None
                    for j in range(4):
                        o = banks[bo][:, j * 128:(j + 1) * 128]
                        ins = pe.matmul(o, lhsT=vt[lp][:, h * 512 + j * 128:h * 512 + (j + 1) * 128], rhs=atbf[:, h, :],
                                        start=True, stop=(tt == 0))
                        if tt > 0:
                            for i in range(2):
                                ins = pe.matmul(o, lhsT=stbf[:, 2 * h + i, j * 128:(j + 1) * 128], rhs=qdec[:, 2 * h + i, :],
                                                start=False, stop=(i == 1))
                    return ins
                sc.add("pe", fnO, reads=[("vt", lp), ("atbf", p), ("qdec", p), ("stbf", 2 * h), ("stbf", 2 * h + 1)], writes=[("bank", bo)])
                act(osq[hp][:].rearrange("p j c -> p (j c)"), banks[bo][:], AF.Square, [("bank", bo)], [("osq", hp)])
                if tt < NT - 1:
                    for i in range(2):
                        dch = 2 * h + i
                        bs = 5 + i
                        mm(banks[bs][:], [(kendT[:, dch * 128:(dch + 1) * 128], vt[lp][:, h * 512:(h + 1) * 512])],
                           [("kendT", p), ("vt", lp)], bs)
                        if tt == 0:
                            dve(lambda e, bs=bs, dch=dch: e.tensor_copy(out=st32[:, dch, :], in_=banks[bs][:]), [("bank", bs)], [("st32", dch)])
                        else:
                            dve(lambda e, bs=bs, dch=dch: e.scalar_tensor_tensor(out=st32[:, dch, :], in0=st32[:, dch, :],
                                                                                  scalar=eb[:, dch, 127:128], in1=banks[bs][:],
                                                                                  op0=ALU.mult, op1=ALU.add),
                                [("bank", bs), ("eb", e3, dch // 4), ("st32", dch)], [("st32", dch)])
                        sc.add("pool", lambda e, dch=dch: e.tensor_copy(out=stbf[:, dch, :], in_=st32[:, dch, :]),
                               reads=[("st32", dch)], writes=[("stbf", dch)])

                def part2():
                    bn = 7
                    mm(banks[bn][:, 0:128], [(ones_512[:], osq[hp][:, j, :]) for j in range(4)], [("osq", hp)], bn)
                    act(rstd[:], banks[bn][:, 0:128], AF.Ln, [("bank", bn)], [("g_rstd", hp)], bias=EPS)
                    act(rstd[:], rstd[:], AF.Exp, [("g_rstd", hp)], [("g_rstd", hp)], scale=-0.5)
                    ra = rstd[:]
                    rb = bass.AP(ra.tensor, ra.offset, [list(ra.ap[0]), [0, 4], list(ra.ap[1])])
                    dve(lambda e: e.tensor_tensor(out=t1[:], in0=banks[bo][:].rearrange("p (j c) -> p j c", j=4), in1=rb, op=ALU.mult),
                        [("bank", bo), ("g_rstd", hp)], [("t1", hp)])
                    dve(lambda e: e.tensor_tensor(out=oT4[og][:, h * 4:(h + 1) * 4, osl], in0=t1[:], in1=r32[lp][:, h * 4:(h + 1) * 4, :], op=ALU.mult),
                        [("t1", hp), ("r32", lp)], [("oT", og, tt % 4, h)])
                    if h == 3 and tt % 4 == 3:
                        g0 = (tt // 4) * 512
                        for hh in range(2):
                            dma("pool", OT[hh * 8:(hh + 1) * 8, :, g0:g0 + 512].rearrange("c p t -> p c t"), oT4[og][:, hh * 8:(hh + 1) * 8, :],
                                reads=[("oT", og, t4, h_) for t4 in range(4) for h_ in range(4)], writes=[("OT", tt // 4, hh)])
                return part2

            loads(0)
            loads(1)
            S0 = front_stages(0)
            for k_ in ("F1", "F2", "F3q", "F3k", "F3b", "F4"):
                S0[k_]()
            if NT > 1:
                S1 = front_stages(1)
                S1["F1"]()
                S1["F2"]()
            deferred = None
            for tt in range(NT):
                Sn = front_stages(tt + 1) if tt + 1 < NT else None
                Sn2 = front_stages(tt + 2) if tt + 2 < NT else None
                for h in range(4):
                    p2 = back_head(tt, h)
                    if deferred is not None:
                        deferred()
                    deferred = p2
                    if h == 0:
                        if Sn:
                            Sn["F3q"]()
                        if tt + 2 < NT:
                            loads(tt + 2)
                    elif h == 1:
                        if Sn:
                            Sn["F3k"]()
                    elif h == 2:
                        if Sn:
                            Sn["F3b"]()
                        if Sn2:
                            Sn2["F1"]()
                    else:
                        if Sn:
                            Sn["F4"]()
                        if Sn2:
                            Sn2["F2"]()
            deferred()
        sc.barrier()

    def phase_ATT():
        with contextlib.ExitStack() as st:
            def t(name, shape, dt=F32):
                return st.enter_context(sbt(name, list(shape), dt))
            oT = t("a_oT", [128, NCH, S], BF16)
            qn = [t("a_q%d" % i, [128, S], BF16) for i in range(2)]
            kn = [t("a_k%d" % i, [128, S], BF16) for i in range(2)]
            vh = [t("a_v%d" % i, [128, 16, 128], BF16) for i in range(2)]
            hk = [t("a_hk%d" % i, [128, 640]) for i in range(2)]
            bt = [t("a_bt%d" % i, [128, 640], BF16) for i in range(2)]
            mB = t("a_mB", [128, 640])
            NR = 8
            pt = [t("a_pt%d" % i, [128, 640], BF16) for i in range(NR)]
            rden = [t("a_rd%d" % i, [128, 512]) for i in range(2)]
            dma("sp", mB[:], cmaskB_d, writes=["mB"])
            NJ = S // 128
            gctr = [0]
            octr = [0]

            def head(h):
                p = h % 2
                dma("sp", qn[p][:], QN[h], writes=[("qn", p)])
                dma("sp", kn[p][:], KN[h], writes=[("kn", p)])
                dma("sp", vh[p][:], VTOK[:, h * 128:(h + 1) * 128].rearrange("(t p) d -> p t d", p=128), writes=[("vh", p)])
                hsrc = bass.AP(extr_t, h * 768, [[1, 128], [128, 5], [1, 128]])
                dma("sp", hk[p][:].rearrange("p (r c) -> p r c", r=5), hsrc, writes=[("hk", p)])
                dve(lambda e: e.tensor_tensor(out=bt[p][:], in0=hk[p][:], in1=mB[:], op=ALU.add), [("hk", p), "mB"], [("bt", p)])
                slot_of = {}

                def qk(kt):
                    g = gctr[0]
                    gctr[0] += 1
                    ps_ = g % NR
                    slot_of[kt] = ps_
                    nq = min(5, NJ - kt)
                    nA = min(nq, 4)
                    bA = 4 + 2 * (g % 2)
                    bB = bA + 1
                    c0 = kt * 128

                    def fnS(pe):
                        pe.matmul(banks[bA][:, 0:nA * 128], lhsT=kn[p][:, c0:c0 + 128], rhs=qn[p][:, c0:c0 + nA * 128], start=True, stop=False)
                        ins = pe.matmul(banks[bA][:, 0:nA * 128], lhsT=anti_bf[:], rhs=bt[p][:, 0:nA * 128], start=False, stop=True)
                        if nq == 5:
                            pe.matmul(banks[bB][:, 0:128], lhsT=kn[p][:, c0:c0 + 128], rhs=qn[p][:, c0 + 512:c0 + 640], start=True, stop=False)
                            ins = pe.matmul(banks[bB][:, 0:128], lhsT=anti_bf[:], rhs=bt[p][:, 512:640], start=False, stop=True)
                        return ins
                    sc.add("pe", fnS, reads=[("kn", p), ("qn", p), ("bt", p)], writes=[("bank", bA), ("bank", bB)])
                    act(pt[ps_][:, 0:nA * 128], banks[bA][:, 0:nA * 128], AF.Exp, [("bank", bA)], [("pt", ps_, 0)])
                    if nq == 5:
                        act(pt[ps_][:, 512:640], banks[bB][:, 0:128], AF.Exp, [("bank", bB)], [("pt", ps_, 1)])

                def pv(j):
                    oi = octr[0] // 4
                    jj = octr[0] % 4
                    octr[0] += 1
                    ob = 2 * (oi % 2)
                    db = ob + 1
                    kts = list(range(max(0, j - 4), j + 1))

                    def fnV(pe):
                        ins = None
                        for i, kt in enumerate(kts):
                            r = j - kt
                            rhs = pt[slot_of[kt]][:, r * 128:(r + 1) * 128]
                            pe.matmul(banks[ob][:, jj * 128:(jj + 1) * 128], lhsT=vh[p][:, kt, :], rhs=rhs, start=(i == 0), stop=(i == len(kts) - 1))
                        for i, kt in enumerate(kts):
                            r = j - kt
                            rhs = pt[slot_of[kt]][:, r * 128:(r + 1) * 128]
                            ins = pe.matmul(banks[db][:, jj * 128:(jj + 1) * 128], lhsT=ones_1[:], rhs=rhs, start=(i == 0), stop=(i == len(kts) - 1))
                        return ins
                    rd = [("pt", slot_of[kt], 1 if (j - kt) == 4 else 0) for kt in kts]
                    sc.add("pe", fnV, reads=[("vh", p)] + rd, writes=[("bank", ob), ("bank", db)])
                    if jj == 3:
                        j0 = j - 3
                        rdn = rden[oi % 2]
                        act(rdn[:], banks[db][:], AF.Ln, [("bank", db)], [("rden", oi % 2)])
                        act(rdn[:], rdn[:], AF.Exp, [("rden", oi % 2)], [("rden", oi % 2)], scale=-1.0)
                        dve(lambda e: e.tensor_tensor(out=oT[:, h, j0 * 128:(j0 + 4) * 128], in0=banks[ob][:], in1=rdn[:], op=ALU.mult),
                            [("bank", ob), ("rden", oi % 2)], [("oT", h, j0)])

                qk(0)
                for kt in range(NJ):
                    if kt + 1 < NJ:
                        qk(kt + 1)
                    pv(kt)

            for h in range(16):
                head(h)
            for hh in range(2):
                dma("sp", OT[hh * 8:(hh + 1) * 8].rearrange("c p t -> p c t"), oT[:, hh * 8:(hh + 1) * 8, :],
                    reads=[("oT", h, j0) for h in range(hh * 8, hh * 8 + 8) for j0 in range(0, NJ, 4)], writes=[("OT", hh)])
        sc.barrier()

    def phase_P3(l, last):
        with contextlib.ExitStack() as st:
            x_fm = st.enter_context(sbt("p3x", [128, NCH, TB], F32))
            a = st.enter_context(sbt("p3a", [128, NCH, TB], BF16))
            u = st.enter_context(sbt("p3u", [128, NCH, TB], BF16))
            sq = [st.enter_context(sbt("p3sq%d" % i, [128, 512], BF16)) for i in range(2)]
            rstd = st.enter_context(sbt("p3rs", [128, TB], F32))
            rl = [st.enter_context(sbt("p3rl%d" % i, [128, 512], F32)) for i in range(2)]
            yst = [st.enter_context(sbt("p3y%d" % i, [128, D], F32)) for i in range(2)] if last else None
            rlc = [0]
            b6 = [0]

            def nb6():
                b6[0] = (b6[0] + 1) % 6
                return b6[0]
            sq4 = sq + [st.enter_context(sbt("p3sq%d" % (i + 2), [128, 512], BF16)) for i in range(2)]
            sqc = [0]
            pend = []

            def flushp():
                while pend:
                    pend.pop(0)()
            for tb in range(NTB):
                t0 = tb * TB
                def load_a(tb_):
                    for cg in range(4):
                        dma("sp", a[:, cg * 4:(cg + 1) * 4, :], OT[cg * 4:(cg + 1) * 4, :, tb_ * TB:(tb_ + 1) * TB].rearrange("c p t -> p c t"),
                            writes=[("act", c, th) for c in range(cg * 4, cg * 4 + 4) for th in range(NTH)])
                if tb == 0:
                    load_a(0)
                for cg in range(4):
                    dma("sp", x_fm[:, cg * 4:(cg + 1) * 4, :], XT[cg * 4:(cg + 1) * 4, :, t0:t0 + TB].rearrange("c p t -> p c t"),
                        writes=[("x", c, th) for c in range(cg * 4, cg * 4 + 4) for th in range(NTH)])
                areads = {th: [("act", k, th) for k in range(NCH)] for th in range(NTH)}
                ureads = {th: [("u", k, th) for k in range(NCH)] for th in range(NTH)}
                for pc in range(8):
                    ws = wnext(("out", l, tb, pc))
                    W = wslots[ws]
                    for cc in range(2):
                        c = pc * 2 + cc
                        for th in range(NTH):
                            ts = slice(th * 512, (th + 1) * 512)
                            b = nb6()
                            mm(banks[b][:], [(W[:, k, cc * 128:(cc + 1) * 128], a[:, k, ts]) for k in range(NCH)], areads[th] + [("w", ws)], b)
                            dve(lambda e, b=b, c=c, ts=ts: e.tensor_tensor(out=x_fm[:, c, ts], in0=x_fm[:, c, ts], in1=banks[b][:], op=ALU.add),
                                [("bank", b), ("x", c, th)], [("x", c, th)])
                            s_ = sqc[0] % 4
                            sqc[0] += 1
                            act(sq4[s_][:], x_fm[:, c, ts], AF.Square, [("x", c, th)], [("sq", s_)])
                            flushp()

                            def ssq(s_=s_, c=c, th=th):
                                sc.add("pe", lambda pe: pe.matmul(banks[6 + th][:], lhsT=ones_d[:], rhs=sq4[s_][:], start=(c == 0), stop=(c == NCH - 1)),
                                       reads=[("sq", s_)], writes=[("bank", 6 + th)])
                            pend.append(ssq)
                flushp()
                for th in range(NTH):
                    ts = slice(th * 512, (th + 1) * 512)
                    act(rstd[:, ts], banks[6 + th][:], AF.Ln, [("bank", 6 + th)], [("rstd", th)], bias=EPS)
                    act(rstd[:, ts], rstd[:, ts], AF.Exp, [("rstd", th)], [("rstd", th)], scale=-0.5)
                    for c in range(NCH):
                        gb = GC_MLP[l] + c
                        dve(lambda e, c=c, ts=ts, gb=gb: e.scalar_tensor_tensor(out=a[:, c, ts], in0=x_fm[:, c, ts], scalar=gcols[:, gb:gb + 1],
                                                                                 in1=rstd[:, ts], op0=ALU.mult, op1=ALU.mult),
                            [("x", c, th), ("rstd", th)], [("act", c, th)])
                for q in range(4):
                    for pc in range(8):
                        ws = wnext(("up", l, tb, q, pc))
                        W = wslots[ws]
                        for cc in range(2):
                            j = pc * 2 + cc
                            for th in range(NTH):
                                ts = slice(th * 512, (th + 1) * 512)
                                b = nb6()
                                mm(banks[b][:], [(W[:, k, cc * 128:(cc + 1) * 128], a[:, k, ts]) for k in range(NCH)], areads[th] + [("w", ws)], b)
                                r_ = rlc[0] % 2
                                rlc[0] += 1
                                act(rl[r_][:], banks[b][:], AF.Relu, [("bank", b)], [("rl", r_)])
                                dve(lambda e, r_=r_, j=j, ts=ts: e.tensor_tensor(out=u[:, j, ts], in0=rl[r_][:], in1=rl[r_][:], op=ALU.mult),
                                    [("rl", r_)], [("u", j, th)])
                    for pc in range(8):
                        ws = wnext(("dn", l, tb, q, pc))
                        W = wslots[ws]
                        for cc in range(2):
                            c = pc * 2 + cc
                            for th in range(NTH):
                                ts = slice(th * 512, (th + 1) * 512)
                                b = nb6()
                                mm(banks[b][:], [(W[:, k, cc * 128:(cc + 1) * 128], u[:, k, ts]) for k in range(NCH)], ureads[th] + [("w", ws)], b)
                                dve(lambda e, b=b, c=c, ts=ts: e.tensor_tensor(out=x_fm[:, c, ts], in0=x_fm[:, c, ts], in1=banks[b][:], op=ALU.add),
                                    [("bank", b), ("x", c, th)], [("x", c, th)])
                if tb + 1 < NTB:
                    load_a(tb + 1)
                if not last:
                    for cg in range(4):
                        dma("sp", XT[cg * 4:(cg + 1) * 4, :, t0:t0 + TB].rearrange("c p t -> p c t"), x_fm[:, cg * 4:(cg + 1) * 4, :],
                            reads=[("x", c, th) for c in range(cg * 4, cg * 4 + 4) for th in range(NTH)], writes=[("XTo", tb, cg)])
                else:
                    for tt in range(TB // 128):
                        th = tt // 4
                        yp = tt % 2
                        for cq in range(4):
                            b = nb6()

                            def fnY(pe, b=b, cq=cq, tt=tt):
                                ins = None
                                for i in range(4):
                                    c = cq * 4 + i
                                    ins = pe.transpose(out=banks[b][:, i * 128:(i + 1) * 128], in_=x_fm[:, c, tt * 128:(tt + 1) * 128], identity=ident[:])
                                return ins
                            sc.add("pe", fnY, reads=[("x", cq * 4 + i, th) for i in range(4)], writes=[("bank", b)])
                            if cq % 2 == 0:
                                act(yst[yp][:, cq * 512:(cq + 1) * 512], banks[b][:], AF.Copy, [("bank", b)], [("yst", yp, cq)])
                            else:
                                dve(lambda e, b=b, yp=yp, cq=cq: e.tensor_copy(out=yst[yp][:, cq * 512:(cq + 1) * 512], in_=banks[b][:]),
                                    [("bank", b)], [("yst", yp, cq)])
                        r0 = t0 + tt * 128
                        dma("sp", y_out[r0:r0 + 128, :], yst[yp][:], reads=[("yst", yp, cq) for cq in range(4)], writes=[("y", r0)])
        sc.barrier()

    phases = [phase_A, lambda: phase_P1(0), phase_GLA, lambda: phase_P3(0, False),
              lambda: phase_P1(1), phase_ATT, lambda: phase_P3(1, True)]
    if only == "gla":
        phases = [phase_GLA]
    if only == "att":
        phases = [phase_ATT]
    for i, ph in enumerate(phases):
        if i >= stop_after:
            break
        ph()
    sc.emit(nc, stack)
    stack.close()
    return nc


def host_consts():
    c = {}
    c["c_ident"] = np.eye(128, dtype=np.float32)
    c["c_anti"] = np.ascontiguousarray(np.eye(128, dtype=np.float32)[::-1])
    s = np.arange(128)[:, None]
    t = np.arange(128)[None, :]
    c["c_ltri"] = np.where(s <= t, np.float32(-1.0 / 16.0), np.float32(0.0)).astype(np.float32)
    c["c_maskA"] = np.tile((s <= t).astype(np.float32), (1, 4))
    k = np.arange(128)[:, None, None]
    r = np.arange(5)[None, :, None]
    cq = np.arange(128)[None, None, :]
    m = 127 - k
    diff = 2 * r + (cq >= 64).astype(np.int64) - (m >= 64).astype(np.int64)
    c["c_maskB"] = np.where((diff >= 0) & (diff <= 8), np.float32(0.0), np.float32(NEG)).astype(np.float32).reshape(128, 640)
    c["c_onesrow"] = np.ones((1, S), dtype=np.float32)
    return c


_CACHE = {}


def kernel(x, norm_mix_g, norm_mlp_g, gla_w_in, gla_w_gate_up, gla_b_gate, gla_g_out, gla_w_out,
           att_w_in, att_g_q, att_g_k, att_rel_bias, att_w_out, mlp_w_up, mlp_w_down):
    f = lambda a: np.ascontiguousarray(np.asarray(a, dtype=np.float32))
    x = f(x)
    def cols(v):
        return np.ascontiguousarray(f(v).reshape(-1, 128).T)
    nm = f(norm_mix_g)
    nl = f(norm_mlp_g)
    gcols = np.concatenate([cols(nm[0]), cols(nl[0]), cols(nm[1]), cols(nl[1]), cols(f(gla_g_out)[0]),
                            cols(f(att_g_q)[0]), cols(f(att_g_k)[0])], axis=1)
    assert gcols.shape == (128, 82)
    idx = np.clip(np.arange(768) - 127, -63, 256) + 63
    ext = np.ascontiguousarray(f(att_rel_bias)[0][:, idx])
    shared = {
        "gla_w_in": f(gla_w_in)[0], "gla_w_gate_up": f(gla_w_gate_up)[0], "gla_b_gate": f(gla_b_gate).reshape(1, 1024),
        "gla_w_out": f(gla_w_out)[0], "att_w_in": f(att_w_in)[0], "att_w_out": f(att_w_out)[0],
        "mlp_w_up": f(mlp_w_up), "mlp_w_down": f(mlp_w_down), "att_ext": ext, "gcols": np.ascontiguousarray(gcols),
    }
    shared.update(host_consts())
    if "nc" not in _CACHE:
        _CACHE["nc"] = build()
    nc = _CACHE["nc"]
    in_maps = []
    for b in range(8):
        m = dict(shared)
        m["x"] = x[b]
        in_maps.append(m)
    res = run_bass_kernel_spmd(nc, in_maps, core_ids=list(range(8)))
    return np.stack([np.asarray(r["y"], dtype=np.float32) for r in res.results], axis=0)
```
